# Optimizing a Trainium2 kernel written in Bass

```python
import jax, jax.numpy as jnp
from jax import lax
import numpy as np

D_MODEL = 1024
BATCH = 16
SEQ = 4096
DEPTH = 4

CHUNK = 64
POOL_WINDOWS = (2, 4, 8, 16)
POOL_GROUPS = len(POOL_WINDOWS)
D_POOL = D_MODEL // 2
POOL_GW = D_POOL // POOL_GROUPS
D_CONV = D_MODEL // 2
CONV_WIDTH = 31
D_IN_EVEN = D_POOL + 2 * D_CONV
D_MIX_EVEN = D_POOL + D_CONV
SGU_LEN = 2 * CHUNK
SGU_HEADS = 4
D_SGU = D_MODEL
SGU_HD = D_SGU // SGU_HEADS
D_FF = 2816
FFN_CONV_WIDTH = 3
N_EVEN = (DEPTH + 1) // 2
N_ODD = DEPTH // 2
EPS = 1e-6

kernel_name = "pool_conformer_sgu_convffn_hybrid"


def rms_norm(x, g):
    xf = x.astype(jnp.float32)
    y = xf * lax.rsqrt(jnp.mean(xf * xf, axis=-1, keepdims=True) + EPS)
    return (y * g.astype(jnp.float32)).astype(x.dtype)


def layer_norm(x, g, b):
    xf = x.astype(jnp.float32)
    mu = jnp.mean(xf, axis=-1, keepdims=True)
    xc = xf - mu
    var = jnp.mean(xc * xc, axis=-1, keepdims=True)
    y = xc * lax.rsqrt(var + EPS) * g.astype(jnp.float32) + b.astype(jnp.float32)
    return y.astype(x.dtype)


def causal_dwconv(x, w, b):
    k, c = w.shape
    y = lax.conv_general_dilated(
        x, w.astype(x.dtype)[:, None, :], window_strides=(1,), padding=[(k - 1, 0)],
        dimension_numbers=("NWC", "WIO", "NWC"), feature_group_count=c)
    return y + b.astype(x.dtype)


def multiscale_pool_residual(u):
    s = u.shape[1]
    uf = u.astype(jnp.float32)
    cs = jnp.pad(jnp.cumsum(uf, axis=1), ((0, 0), (1, 0), (0, 0), (0, 0)))
    pos = jnp.arange(s)
    outs = []
    for g, w in enumerate(POOL_WINDOWS):
        c = cs[:, :, g]
        lower = jnp.pad(c, ((0, 0), (w - 1, 0), (0, 0)))[:, :s]
        cnt = jnp.minimum(pos + 1, w).astype(jnp.float32)[None, :, None]
        outs.append((c[:, 1:] - lower) / cnt)
    pooled = jnp.stack(outs, axis=2)
    return (pooled - uf).astype(u.dtype)


def pool_conv_mixer(h, w_in, pool_w, pool_scale, conv_w, conv_b, cn_g, cn_b, w_out):
    bsz, s, _ = h.shape
    z = h @ w_in
    za = z[..., :D_POOL]
    zb_val = z[..., D_POOL:D_POOL + D_CONV]
    zb_gate = z[..., D_POOL + D_CONV:]
    pa = multiscale_pool_residual(za.reshape(bsz, s, POOL_GROUPS, POOL_GW))
    ya = jnp.einsum("bsgc,gcd->bsgd", pa, pool_w).reshape(bsz, s, D_POOL) * pool_scale
    gl = zb_val * jax.nn.sigmoid(zb_gate)
    yb = jax.nn.silu(layer_norm(causal_dwconv(gl, conv_w, conv_b), cn_g, cn_b))
    return jnp.concatenate([ya, yb], axis=-1) @ w_out


def sgu_mixer(h, w_in, vn_g, vn_b, w_s, b_s, w_out):
    bsz, s, _ = h.shape
    z = jax.nn.gelu(h @ w_in, approximate=False)
    u, v = jnp.split(z, 2, axis=-1)
    v = layer_norm(v, vn_g, vn_b)
    n = s // SGU_LEN
    v = v.reshape(bsz, n, SGU_LEN, SGU_HEADS, SGU_HD)
    mask = jnp.tril(jnp.ones((SGU_LEN, SGU_LEN), dtype=bool))
    ws = jnp.where(mask[None], w_s, jnp.zeros_like(w_s)).astype(v.dtype)
    sv = jnp.einsum("hqp,bnphd->bnqhd", ws, v) + b_s.T.astype(v.dtype)[None, None, :, :, None]
    return (u * sv.reshape(bsz, s, D_SGU)) @ w_out


def conv_ffn(h, w_up, conv_w, conv_b, w_down):
    z = causal_dwconv(h @ w_up, conv_w, conv_b)
    a, g = jnp.split(z, 2, axis=-1)
    return (jax.nn.silu(g) * a) @ w_down


def setup_inputs(seed: int = 0) -> dict:
    key = jax.random.key(seed)
    ks = iter(jax.random.split(key, 32))

    def nrm(shape, scale):
        return jax.random.normal(next(ks), shape, jnp.float32) * scale

    def gain(shape):
        return 1.0 + nrm(shape, 0.02)

    return {
        "x": nrm((BATCH, SEQ, D_MODEL), 1.0),
        "ev_w_in": nrm((N_EVEN, D_MODEL, D_IN_EVEN), D_MODEL ** -0.5),
        "ev_pool_w": nrm((N_EVEN, POOL_GROUPS, POOL_GW, POOL_GW), POOL_GW ** -0.5),
        "ev_pool_scale": gain((N_EVEN, D_POOL)),
        "ev_conv_w": nrm((N_EVEN, CONV_WIDTH, D_CONV), CONV_WIDTH ** -0.5),
        "ev_conv_b": nrm((N_EVEN, D_CONV), 0.02),
        "ev_cn_g": gain((N_EVEN, D_CONV)),
        "ev_cn_b": nrm((N_EVEN, D_CONV), 0.02),
        "ev_w_out": nrm((N_EVEN, D_MIX_EVEN, D_MODEL), D_MIX_EVEN ** -0.5),
        "od_w_in": nrm((N_ODD, D_MODEL, 2 * D_SGU), D_MODEL ** -0.5),
        "od_vn_g": gain((N_ODD, D_SGU)),
        "od_vn_b": nrm((N_ODD, D_SGU), 0.02),
        "od_w_s": nrm((N_ODD, SGU_HEADS, SGU_LEN, SGU_LEN), SGU_LEN ** -0.5),
        "od_b_s": nrm((N_ODD, SGU_HEADS, SGU_LEN), 0.02),
        "od_w_out": nrm((N_ODD, D_SGU, D_MODEL), D_SGU ** -0.5),
        "ffn_w_up": nrm((DEPTH, D_MODEL, 2 * D_FF), D_MODEL ** -0.5),
        "ffn_conv_w": nrm((DEPTH, FFN_CONV_WIDTH, 2 * D_FF), FFN_CONV_WIDTH ** -0.5),
        "ffn_conv_b": nrm((DEPTH, 2 * D_FF), 0.02),
        "ffn_w_down": nrm((DEPTH, D_FF, D_MODEL), D_FF ** -0.5),
        "mix_norm_g": gain((DEPTH, D_MODEL)),
        "ffn_norm_g": gain((DEPTH, D_MODEL)),
        "final_norm_g": gain((D_MODEL,)),
    }


def reference(x, ev_w_in, ev_pool_w, ev_pool_scale, ev_conv_w, ev_conv_b, ev_cn_g, ev_cn_b,
              ev_w_out, od_w_in, od_vn_g, od_vn_b, od_w_s, od_b_s, od_w_out,
              ffn_w_up, ffn_conv_w, ffn_conv_b, ffn_w_down,
              mix_norm_g, ffn_norm_g, final_norm_g):
    for l in range(DEPTH):
        h = rms_norm(x, mix_norm_g[l])
        i = l // 2
        if l % 2 == 0:
            x = x + pool_conv_mixer(h, ev_w_in[i], ev_pool_w[i], ev_pool_scale[i],
                                    ev_conv_w[i], ev_conv_b[i], ev_cn_g[i], ev_cn_b[i],
                                    ev_w_out[i])
        else:
            x = x + sgu_mixer(h, od_w_in[i], od_vn_g[i], od_vn_b[i], od_w_s[i], od_b_s[i],
                              od_w_out[i])
        x = x + conv_ffn(rms_norm(x, ffn_norm_g[l]), ffn_w_up[l], ffn_conv_w[l],
                         ffn_conv_b[l], ffn_w_down[l])
    return rms_norm(x, final_norm_g)
```

```python
import bisect
import numpy as np
import concourse.bass as bass
import concourse.mybir as mybir
from concourse.bass_utils import run_bass_kernel_spmd

F32 = mybir.dt.float32
BF16 = mybir.dt.bfloat16
AF = mybir.ActivationFunctionType
ALU = mybir.AluOpType

PE, ACT, DVE, POOL, SP = "pe", "act", "dve", "pool", "sp"
ENGINES = (PE, ACT, DVE, POOL, SP)

D = 1024
KT = 8
DFF = 2816
FT = 22
EPS = 1e-6
T = 1024
NS = 512
NSUB = T // NS
CW = 31
PH = 15
CH = 30
RING_SLOTS = 4
SLOT_BYTES = 8192


class IMap:
    def __init__(self):
        self.st = []
        self.segs = []

    def _split(self, pos):
        i = bisect.bisect_right(self.st, pos) - 1
        if i >= 0 and self.st[i] < pos < self.segs[i][0]:
            end, w, r = self.segs[i]
            self.segs[i] = [pos, w, dict(r)]
            self.st.insert(i + 1, pos)
            self.segs.insert(i + 1, [end, w, dict(r)])

    def access(self, lo, hi, tok, write, deps):
        self._split(lo)
        self._split(hi)
        i = bisect.bisect_left(self.st, lo)
        j = i
        n = len(self.st)
        while j < n and self.st[j] < hi:
            seg = self.segs[j]
            if seg[1] is not None:
                deps.append(seg[1])
            if write:
                for k, v in seg[2].items():
                    deps.append((k, v))
            j += 1
        if write:
            del self.st[i:j]
            del self.segs[i:j]
            self.st.insert(i, lo)
            self.segs.insert(i, [hi, tok, {}])
        else:
            pos = lo
            k = i
            while k < j:
                s = self.st[k]
                if s > pos:
                    self.st.insert(k, pos)
                    self.segs.insert(k, [s, None, {tok[0]: tok[1]}])
                    k += 1
                    j += 1
                seg = self.segs[k]
                rd = seg[2]
                if rd.get(tok[0], -1) < tok[1]:
                    rd[tok[0]] = tok[1]
                pos = seg[0]
                k += 1
            if pos < hi:
                self.st.insert(k, pos)
                self.segs.insert(k, [hi, None, {tok[0]: tok[1]}])


    def collect(self, lo, hi, deps):
        i = bisect.bisect_right(self.st, lo) - 1
        if i < 0:
            i = 0
        n = len(self.st)
        while i < n and self.st[i] < hi:
            seg = self.segs[i]
            if seg[0] > lo:
                if seg[1] is not None:
                    deps.append(seg[1])
                for k, v in seg[2].items():
                    deps.append((k, v))
            i += 1


class V:
    __slots__ = ("ap", "space", "ivs")

    def __init__(self, ap, space, ivs):
        self.ap = ap
        self.space = space
        self.ivs = ivs


class Tl:
    def __init__(self, ap, space, off, shape, es):
        self.ap = ap
        self.space = space
        self.off = off
        self.shape = tuple(shape)
        self.es = es
        self.nbytes = int(np.prod(shape)) * es

    def __getitem__(self, idx):
        if not isinstance(idx, tuple):
            idx = (idx,)
        idx = list(idx) + [slice(None)] * (len(self.shape) - len(idx))
        rng = []
        for d, ix in zip(self.shape, idx):
            if isinstance(ix, slice):
                lo = 0 if ix.start is None else ix.start
                hi = d if ix.stop is None else ix.stop
                assert 0 <= lo < hi <= d, (lo, hi, d, self.shape)
                rng.append((lo, hi))
            else:
                assert 0 <= ix < d, (ix, d)
                rng.append((ix, ix + 1))
        ap = self.ap[(slice(None),) + tuple(idx)]
        strides = []
        s = self.es
        for d in reversed(self.shape):
            strides.append(s)
            s *= d
        strides = strides[::-1]
        nd = len(self.shape)
        k = nd
        while k > 0 and rng[k - 1] == (0, self.shape[k - 1]):
            k -= 1
        if k == 0:
            ivs = [(self.off, self.off + self.nbytes)]
        else:
            ivs = []
            inner = strides[k - 1]
            lo, hi = rng[k - 1]

            def rec(dim, base):
                if dim == k - 1:
                    ivs.append((base + lo * inner, base + hi * inner))
                    return
                for i in range(rng[dim][0], rng[dim][1]):
                    rec(dim + 1, base + i * strides[dim])
            rec(0, self.off)
        return V(ap, self.space, ivs)

    def full(self):
        return V(self.ap, self.space, [(self.off, self.off + self.nbytes)])

    def rows(self, lo, hi):
        t = Tl(self.ap[lo:hi], self.space, self.off, self.shape, self.es)
        return t


class Op:
    __slots__ = ("eng", "fn", "deps", "milestone", "dma_sem", "dma_n", "idx")


class Builder:
    def __init__(self, nc):
        self.nc = nc
        self.ops = {e: [] for e in ENGINES}
        self.maps = {}
        self.dma_counts = {}

    def _imap(self, space):
        m = self.maps.get(space)
        if m is None:
            m = self.maps[space] = IMap()
        return m

    def op(self, eng, fn, reads=(), writes=(), dma_sem=None, dma_n=0):
        o = Op()
        o.eng = eng
        o.fn = fn
        o.idx = len(self.ops[eng])
        o.milestone = False
        o.dma_sem = dma_sem
        o.dma_n = dma_n
        if dma_sem is not None:
            c = self.dma_counts.get(dma_sem, 0) + 16 * dma_n
            self.dma_counts[dma_sem] = c
            tok = (("dma", dma_sem), c)
        else:
            tok = (("eng", eng), o.idx)
        deps = []
        for v in reads:
            m = self._imap(v.space)
            for lo, hi in v.ivs:
                m.access(lo, hi, tok, False, deps)
        for v in writes:
            m = self._imap(v.space)
            for lo, hi in v.ivs:
                m.access(lo, hi, tok, True, deps)
        red = {}
        for k, val in deps:
            if k == tok[0] and val == tok[1]:
                continue
            if red.get(k, -1) < val:
                red[k] = val
        if dma_sem is None and eng == PE:
            red.pop(("eng", PE), None)
        o.deps = red
        self.ops[eng].append(o)
        return o

    def wait_for(self, eng, views):
        o = Op()
        o.eng = eng
        o.fn = None
        o.idx = len(self.ops[eng])
        o.milestone = False
        o.dma_sem = None
        o.dma_n = 0
        deps = []
        for v in views:
            m = self._imap(v.space)
            for lo, hi in v.ivs:
                m.collect(lo, hi, deps)
        red = {}
        for k, val in deps:
            if red.get(k, -1) < val:
                red[k] = val
        o.deps = red
        self.ops[eng].append(o)
        return o

    def emit(self, sems, dma_sems, block):
        for e in ENGINES:
            for o in self.ops[e]:
                for k, val in o.deps.items():
                    if k[0] == "eng":
                        self.ops[k[1]][val].milestone = True
        mcount = {}
        for e in ENGINES:
            c = 0
            arr = []
            for o in self.ops[e]:
                if o.milestone:
                    c += 1
                arr.append(c)
            mcount[e] = arr
        self.n_inst = {e: len(self.ops[e]) for e in ENGINES}

        def run(e, eng):
            waited = {}
            for o in self.ops[e]:
                for k, val in o.deps.items():
                    if k[0] == "eng":
                        sem = sems[k[1]]
                        target = mcount[k[1]][val]
                    else:
                        sem = dma_sems[k[1]]
                        target = val
                    if waited.get(k, 0) >= target:
                        continue
                    waited[k] = target
                    eng.wait_ge(sem, target)
                if o.fn is None:
                    continue
                ins = o.fn(eng)
                if o.dma_sem is None and o.milestone:
                    ins.then_inc(sems[e], 1)

        @block.tensor
        def _(eng):
            run(PE, eng)

        @block.scalar
        def _(eng):
            run(ACT, eng)

        @block.vector
        def _(eng):
            run(DVE, eng)

        @block.gpsimd
        def _(eng):
            run(POOL, eng)

        @block.sync
        def _(eng):
            run(SP, eng)


class Cfg:
    def __init__(self, n_seq=2, seq=4096, layers=(0, 1, 2, 3), final_norm=True, parts=("mix", "ffn")):
        self.n_seq = n_seq
        self.seq = seq
        self.ntok = n_seq * seq
        self.layers = tuple(layers)
        self.final_norm = final_norm
        self.parts = parts
        assert seq % T == 0


def _vec_layout():
    off = {}
    c = 0
    for name, n in (("mix_g", 32), ("ffn_g", 32), ("fin_g", 8), ("pool_scale", 8), ("conv_b", 8),
                    ("cn_g", 8), ("cn_b", 8), ("conv_w", 2 * 4 * CW), ("fcw", 4 * 3 * 44), ("fcb", 4 * 44)):
        off[name] = c
        c += n
    return off, c


VOFF, NVEC = _vec_layout()


def build_program(cfg):
    nc = bass.Bass("TRN2", target_bir_lowering=False)
    NTOK = cfg.ntok
    dt = nc.dram_tensor
    xT = dt("xT", [128, KT, NTOK], F32, kind="ExternalInput").ap()
    outT = dt("outT", [128, KT, NTOK], F32, kind="ExternalOutput").ap()
    vecs_d = dt("vecs", [128, NVEC], F32, kind="ExternalInput").ap()
    vn_d = dt("vn", [2, 2, D], F32, kind="ExternalInput").ap()
    bs_d = dt("bs", [1, 1024], F32, kind="ExternalInput").ap()
    wsT_d = dt("wsT", [2, 128, 512], F32, kind="ExternalInput").ap()
    poolw_d = dt("poolw", [2, 128, 512], F32, kind="ExternalInput").ap()
    ev_in_d = dt("ev_in", [2, 3, 128, 4096], F32, kind="ExternalInput").ap()
    ev_out_d = dt("ev_out", [2, 2, 128, 4096], F32, kind="ExternalInput").ap()
    od_in_d = dt("od_in", [2, 4, 128, 4096], F32, kind="ExternalInput").ap()
    od_out_d = dt("od_out", [2, 2, 128, 4096], F32, kind="ExternalInput").ap()
    up_d = dt("ffn_up", [4, 11, 128, 4096], F32, kind="ExternalInput").ap()
    dn_d = dt("ffn_dn", [4, 8, 128, 2816], F32, kind="ExternalInput").ap()
    ev_in_s = dt("ev_in_s", [2, 3, 128, 4096], BF16, kind="Internal").ap()
    ev_out_s = dt("ev_out_s", [2, 2, 128, 4096], BF16, kind="Internal").ap()
    od_in_s = dt("od_in_s", [2, 4, 128, 4096], BF16, kind="Internal").ap()
    od_out_s = dt("od_out_s", [2, 2, 128, 4096], BF16, kind="Internal").ap()
    up_s = dt("ffn_up_s", [4, 11, 128, 4096], BF16, kind="Internal").ap()
    dn_s = dt("ffn_dn_s", [4, 8, 128, 2816], BF16, kind="Internal").ap()
    diag_s = dt("diag_s", [2, 4, 128, CW * 128], BF16, kind="Internal").ap()
    poolw_s = dt("poolw_s", [2, 128, 512], BF16, kind="Internal").ap()

    ARENA_BYTES = 207 * 1024
    import contextlib
    es = contextlib.ExitStack()
    arena = es.enter_context(nc.sbuf_tensor("arena", [128, ARENA_BYTES // 4], F32))
    psum = es.enter_context(nc.psum_tensor("psum", [128, 8, 512], F32))
    sems = {e: es.enter_context(nc.semaphore("sem_" + e)) for e in (PE, ACT, DVE, POOL)}
    DMA_SEM_NAMES = ["ring%d" % i for i in range(RING_SLOTS)] + ["xld0", "xld1", "yst0", "yst1", "vec", "wst0", "wst1", "bsr", "vnb0", "vnb1", "sto0", "sto1", "sto2", "sto3"]
    dma_sems = {n: es.enter_context(nc.semaphore("dsem_" + n)) for n in DMA_SEM_NAMES}
    block = es.enter_context(nc.Block())

    B = Builder(nc)
    cursor = [0]

    def alloc(shape, dtype, at=None):
        esz = 4 if dtype == F32 else 2
        nb = int(np.prod(shape)) * esz
        if at is None:
            off = (cursor[0] + 31) // 32 * 32
            cursor[0] = off + nb
            assert cursor[0] <= ARENA_BYTES, ("arena overflow", cursor[0])
        else:
            off = at
        assert off % 4 == 0 and nb % 4 == 0
        ap = arena[:, off // 4:(off + nb) // 4]
        if dtype == BF16:
            ap = ap.bitcast(BF16)
        if len(shape) == 2:
            ap = ap.rearrange("p (a b) -> p a b", a=shape[0])
        elif len(shape) == 3:
            ap = ap.rearrange("p (a b c) -> p a b c", a=shape[0], b=shape[1])
        return Tl(ap, "sb", off, shape, esz)

    banks = [Tl(psum[:, b, :], "ps", b * 2048, (512,), 4) for b in range(8)]
    bank_rr = [0]

    def next_bank():
        b = banks[bank_rr[0] % 6]
        bank_rr[0] += 1
        return b
    SB0, SB1 = banks[6], banks[7]

    X = alloc((KT, T), F32)
    H = alloc((KT, T), BF16)
    BIG_OFF = (cursor[0] + 31) // 32 * 32
    BIG_BYTES = 48 * 1024
    cursor[0] = BIG_OFF + BIG_BYTES
    RING = [alloc((SLOT_BYTES // 2,), BF16) for _ in range(RING_SLOTS)]
    TMP_OFF = (cursor[0] + 31) // 32 * 32
    TMP_BYTES = 25 * 1024
    cursor[0] = TMP_OFF + TMP_BYTES
    SQ = alloc((KT, NS), BF16)
    VEC = alloc((NVEC,), F32)
    ONES = alloc((128,), BF16)
    ONESH = alloc((128,), BF16)
    ONE1 = alloc((128,), BF16)
    MHALF = alloc((8,), F32)
    RCNT = alloc((16,), F32)
    ZSTATE = alloc((2, 4, PH), F32)
    GSTATE = alloc((2, 4, CH), BF16)
    FSTATE = alloc((4, 44, 2), F32)
    WST = alloc((2, 4, 128), BF16)
    BSROW = alloc((4096,), BF16)
    VNB = alloc((2, D), F32)
    STAT = alloc((16,), F32)


    def big(shape, dtype, boff):
        return alloc(shape, dtype, at=BIG_OFF + boff)

    def tmp(shape, dtype, toff):
        return alloc(shape, dtype, at=TMP_OFF + toff)

    def vcol(name, idx):
        c = VOFF[name] + idx
        return VEC[c:c + 1]

    def dma(eng, out_v, src_ap, sem):
        B.op(eng, lambda e: e.dma_start(out=out_v.ap, in_=src_ap).then_inc(dma_sems[sem], 16),
             reads=(), writes=(out_v,), dma_sem=sem, dma_n=1)

    def dma_out(eng, dst_ap, in_v, sem):
        B.op(eng, lambda e: e.dma_start(out=dst_ap, in_=in_v.ap).then_inc(dma_sems[sem], 16),
             reads=(in_v,), writes=(), dma_sem=sem, dma_n=1)

    def act(out_v, in_v, func, scale=None, bias=None, extra_reads=()):
        kw = {}
        rd = [in_v] + list(extra_reads)
        if scale is not None:
            if isinstance(scale, V):
                kw["scale"] = scale.ap
                rd.append(scale)
            else:
                kw["scale"] = scale
        if bias is not None:
            if isinstance(bias, V):
                kw["bias"] = bias.ap
                rd.append(bias)
            else:
                kw["bias"] = bias
        B.op(ACT, lambda e: e.activation(out=out_v.ap, in_=in_v.ap, func=func, **kw), reads=rd, writes=(out_v,))

    def tt(eng, out_v, a_v, b_v, op):
        B.op(eng, lambda e: e.tensor_tensor(out=out_v.ap, in0=a_v.ap, in1=b_v.ap, op=op),
             reads=(a_v, b_v), writes=(out_v,))

    def ts(eng, out_v, a_v, s1, op0, s2=None, op1=None):
        rd = [a_v]
        a1 = s1
        if isinstance(s1, V):
            a1 = s1.ap
            rd.append(s1)
        a2 = s2
        if isinstance(s2, V):
            a2 = s2.ap
            rd.append(s2)
        if op1 is None:
            B.op(eng, lambda e: e.tensor_scalar(out=out_v.ap, in0=a_v.ap, scalar1=a1, scalar2=None, op0=op0),
                 reads=rd, writes=(out_v,))
        else:
            B.op(eng, lambda e: e.tensor_scalar(out=out_v.ap, in0=a_v.ap, scalar1=a1, scalar2=a2, op0=op0, op1=op1),
                 reads=rd, writes=(out_v,))

    def stt(out_v, a_v, s, b_v, op0, op1):
        rd = [a_v, b_v]
        a1 = s
        if isinstance(s, V):
            a1 = s.ap
            rd.append(s)
        B.op(DVE, lambda e: e.scalar_tensor_tensor(out=out_v.ap, in0=a_v.ap, scalar=a1, in1=b_v.ap, op0=op0, op1=op1),
             reads=rd, writes=(out_v,))

    def copy(eng, out_v, in_v):
        B.op(eng, lambda e: e.tensor_copy(out=out_v.ap, in_=in_v.ap), reads=(in_v,), writes=(out_v,))

    def memset(eng, out_v, val):
        B.op(eng, lambda e: e.memset(out_v.ap, val), reads=(), writes=(out_v,))

    def mm_group(out_v, pairs, extra_reads=()):
        rd = []
        for l, r in pairs:
            rd.append(l)
            rd.append(r)
        rd += list(extra_reads)
        n = len(pairs)

        def fn(e):
            ins = None
            for i, (l, r) in enumerate(pairs):
                ins = e.matmul(out_v.ap, lhsT=l.ap, rhs=r.ap, start=(i == 0), stop=(i == n - 1))
            return ins
        B.op(PE, fn, reads=rd, writes=(out_v,))

    n_tiles = NTOK // T
    tiles_per_seq = cfg.seq // T
    has_mix = "mix" in cfg.parts
    has_ffn = "ffn" in cfg.parts
    chunk_ids = {}

    def dram_v(key):
        cid = chunk_ids.setdefault(key, len(chunk_ids))
        return V(None, "dr", [(cid, cid + 1)])

    def wview(slot, k, m0, m1, ncols):
        a = k * ncols + m0
        return slot[a:a + (m1 - m0)]

    def build_diag(i, c):
        def build(slot):
            for k in range(CW):
                w = vcol("conv_w", (i * 4 + c) * CW + k)
                if k % 2 == 0:
                    ts(DVE, slot[k * 128:(k + 1) * 128], IDENT.full(), w, ALU.mult)
                else:
                    act(slot[k * 128:(k + 1) * 128], IDENT.full(), AF.Identity, scale=w)
        return build

    def layer_chunks(l):
        i = l // 2
        res = []
        if has_mix:
            if l % 2 == 0:
                for rep in range(NSUB):
                    for c in range(3):
                        res.append((("ev_in", i, c), ev_in_d[i, c], ev_in_s[i, c], 4096, None, rep == 0))
                for rep in range(NSUB):
                    for c in range(4):
                        res.append((("diag", i, c), None, diag_s[i, c], CW * 128, build_diag(i, c), rep == 0))
                        if rep == 0 and c == 1:
                            res.append((("poolw", i), poolw_d[i], poolw_s[i], 512, None, True))
                for c in range(2):
                    res.append((("ev_out", i, c), ev_out_d[i, c], ev_out_s[i, c], 4096, None, True))
            else:
                for c in range(4):
                    res.append((("od_in", i, c), od_in_d[i, c], od_in_s[i, c], 4096, None, True))
                for c in range(2):
                    res.append((("od_out", i, c), od_out_d[i, c], od_out_s[i, c], 4096, None, True))
        if has_ffn:
            for c in range(11):
                res.append((("up", l, c), up_d[l, c], up_s[l, c], 4096, None, True))
            for rep in range(NSUB):
                for c in range(8):
                    res.append((("dn", l, c), dn_d[l, c], dn_s[l, c], 2816, None, rep == 0))
        return res

    tile_chunks = []
    for l in cfg.layers:
        tile_chunks += layer_chunks(l)
    NCH = len(tile_chunks)
    issued = [0]
    consumed = [0]
    sto_i = [0]
    LOOKAHEAD = 2

    def issue_chunk(n):
        ti, ci = divmod(n, NCH)
        if ti >= n_tiles:
            return
        key, src_ap, scr_ap, nelem, build, do_store = tile_chunks[ci]
        s = n % RING_SLOTS
        slot = RING[s]
        v = slot[0:nelem]
        rs = "ring%d" % s
        half = nelem // 2 if nelem > 2048 else nelem
        nostore = not do_store
        if ti == 0:
            if build is None:
                o_ap = v.ap.rearrange("p (a b) -> p a b", b=half)
                i_ap = src_ap.rearrange("p (a b) -> p a b", b=half)
                B.op(POOL, lambda e: e.dma_start(out=o_ap, in_=i_ap).then_inc(dma_sems[rs], 16),
                     reads=(), writes=(v,), dma_sem=rs, dma_n=1)
            else:
                build(slot)
            if n_tiles > 1 and not nostore:
                j = sto_i[0] % 4
                sto_i[0] += 1
                name = "sto%d" % j
                B.op(SP, lambda e: e.dma_start(out=scr_ap, in_=v.ap).then_inc(dma_sems[name], 16),
                     reads=(v,), writes=(dram_v(key), dram_v(("sem", name))), dma_sem=name, dma_n=1)
        else:
            B.op(SP, lambda e: e.dma_start(out=v.ap, in_=scr_ap).then_inc(dma_sems[rs], 16),
                 reads=(dram_v(key),), writes=(v,), dma_sem=rs, dma_n=1)

    def get_chunk(key):
        n = consumed[0]
        consumed[0] += 1
        assert tile_chunks[n % NCH][0] == key, (tile_chunks[n % NCH][0], key)
        while issued[0] <= n + LOOKAHEAD:
            issue_chunk(issued[0])
            issued[0] += 1
        return RING[n % RING_SLOTS]

    dma(POOL, VEC.full(), vecs_d, "vec")
    memset(DVE, ONES.full(), 1.0 / 1024.0)
    memset(DVE, ONESH.full(), 1.0 / 512.0)
    memset(DVE, ONE1.full(), 1.0)
    memset(DVE, MHALF.full(), -0.5)
    for i in range(16):
        memset(DVE, RCNT[i:i + 1], 1.0 / (i + 1))
    EPSV = alloc((1,), F32)
    memset(DVE, EPSV.full(), EPS)
    IDENT = alloc((128,), F32)
    Y1 = alloc((KT, NS), F32)
    if has_mix:
        STGA = alloc((1024,), F32, at=BIG_OFF)
        ONESF = alloc((128,), F32, at=BIG_OFF + 4096)
        memset(POOL, ONESF.full(), 1.0)
        B.op(POOL, lambda e: e.affine_select(out=IDENT.full().ap, in_=ONESF.full().ap, pattern=[[1, 128]],
                                             compare_op=ALU.is_equal, fill=0.0, base=0, channel_multiplier=-1),
             reads=(ONESF.full(),), writes=(IDENT.full(),))
        for i in sorted(set(l // 2 for l in cfg.layers if l % 2 == 1)):
            dma(SP, STGA[0:512], wsT_d[i], "wst%d" % i)
            for h in range(4):
                src = STGA[h * 128:(h + 1) * 128]
                dstv = WST[i, h]
                B.op(POOL, (lambda src, dstv: (lambda e: e.affine_select(
                    out=dstv.ap, in_=src.ap, pattern=[[1, 128]], compare_op=ALU.is_ge, fill=0.0, base=0,
                    channel_multiplier=-1)))(src, dstv), reads=(src,), writes=(dstv,))
        r0 = STGA.rows(0, 1)
        dma(SP, r0[0:1024], bs_d, "bsr")
        for ih in range(8):
            for q in range(4):
                copy(DVE, BSROW.rows(0, 1)[ih * 512 + q * 128:ih * 512 + (q + 1) * 128], r0[ih * 128:(ih + 1) * 128])

    stat_cnt = [0, 0]
    SBS = [SB0, SB1]

    stat_pend = []

    def stat_update(k, s, defer=False):
        cs = slice(s * NS, (s + 1) * NS)
        sq = SQ[(k + 4 * s) % KT]
        act(sq, X[k, cs], AF.Square)
        c = stat_cnt[s]
        bank = SBS[s]

        def pe_part():
            B.op(PE, lambda e: e.matmul(bank.full().ap, lhsT=ONES.full().ap, rhs=sq.ap, start=(c == 0), stop=(c == KT - 1)),
                 reads=(ONES.full(), sq), writes=(bank.full(),))
        stat_cnt[s] = (c + 1) % KT
        if defer:
            stat_pend.append([int(defer), pe_part, s])
        else:
            pe_part()

    late_ops = []
    first_kind = [None]

    def late_flush():
        if late_ops:
            stat_flush()
        for ent in late_ops:
            ent[1]()
        late_ops[:] = []

    def stat_tick():
        keepl = []
        for ent in late_ops:
            ent[0] -= 1
            if ent[0] <= 0:
                stat_flush()
                ent[1]()
            else:
                keepl.append(ent)
        late_ops[:] = keepl
        keep = []
        for ent in stat_pend:
            ent[0] -= 1
            if ent[0] <= 0:
                ent[1]()
            else:
                keep.append(ent)
        stat_pend[:] = keep

    def stat_flush(s=None):
        keep = []
        for ent in stat_pend:
            if s is None or ent[2] == s:
                ent[1]()
            else:
                keep.append(ent)
        stat_pend[:] = keep

    def rstd_bank(s):
        stat_flush(s)
        assert stat_cnt[s] == 0
        bank = SBS[s]
        act(bank.full(), bank.full(), AF.Ln, bias=EPSV.full())
        act(bank.full(), bank.full(), AF.Exp, scale=-0.5)
        return bank

    def norm_emit(s, kind):
        cs = slice(s * NS, (s + 1) * NS)
        bank = rstd_bank(s)
        if kind[0] == "H":
            _, gname, l = kind
            for k in range(KT):
                stt(H[k, cs], X[k, cs], vcol(gname, l * 8 + k), bank.full(), ALU.mult, ALU.mult)
        else:
            ti = kind[1]
            t0 = ti * T
            if cfg.final_norm:
                for k in range(KT):
                    stt(Y1[k], X[k, cs], vcol("fin_g", k), bank.full(), ALU.mult, ALU.mult)
                src = Y1.full()
            else:
                src = X[:, cs]
            if ti + 1 < n_tiles:
                t1 = (ti + 1) * T
                dma(POOL, X[:, cs], xT[:, :, t1 + s * NS:t1 + (s + 1) * NS], "xld%d" % s)

                def upd(s=s):
                    for k in range(KT):
                        stat_update(k, s)
                late_ops.append([4, upd, s])
                if s == 0:
                    late_ops.append([6, lambda: norm_emit(0, first_kind[0]), 0])
            dma_out(POOL, outT[:, :, t0 + s * NS:t0 + (s + 1) * NS], src, "yst%d" % s)

    def resid_add(dch, s, bank):
        cs = slice(s * NS, (s + 1) * NS)
        tt(DVE, X[dch, cs], bank.full(), X[dch, cs], ALU.add)
        stat_update(dch, s, defer=2)

    def ffn_layer(l, first_in_seq, next_kind, hook):
        AW = T + 2
        ACCL = [tmp((AW,), F32, j * 4112) for j in range(4)]
        SG = [tmp((512,), F32, 4 * 4112 + j * 2048) for j in range(4)]
        U = big((FT, T), BF16, 0)
        tails = []
        it = 0
        for c in range(11):
            slot = get_chunk(("up", l, c))
            accs_f = {}
            for j in range(2):
                f = 2 * c + j
                accs = {}
                for part in (0, 1):
                    ft = f + part * FT
                    acc = ACCL[2 * j + part]
                    accs[part] = acc
                    bb = vcol("fcb", l * 44 + ft)
                    if first_in_seq:
                        copy(POOL, acc[0:1], bb)
                        copy(POOL, acc[1:2], bb)
                    else:
                        copy(POOL, acc[0:2], FSTATE[l, ft])
                accs_f[j] = accs
            for s in range(NSUB):
                if s == 1:
                    run_hook(hook)
                for j in range(2):
                    f = 2 * c + j
                    accs = accs_f[j]
                    cs = slice(s * NS, (s + 1) * NS)
                    o = s * NS
                    bks = {}
                    for part in (0, 1):
                        ft = f + part * FT
                        acc = accs[part]
                        bank = next_bank()
                        bks[part] = bank
                        m0 = part * 256 + j * 128
                        mm_group(bank.full(), [(wview(slot, k, m0, m0 + 128, 512), H[k, cs]) for k in range(KT)])
                        w0 = vcol("fcw", (l * 3 + 0) * 44 + ft)
                        bb = vcol("fcb", l * 44 + ft)
                        act(acc[o + 2:o + 514], bank.full(), AF.Identity, scale=w0, bias=bb)
                        if part == 0:
                            for fn in tails:
                                fn()
                            tails = []
                    for part in (0, 1):
                        ft = f + part * FT
                        acc = accs[part]
                        bank = bks[part]
                        w1 = vcol("fcw", (l * 3 + 1) * 44 + ft)
                        w2 = vcol("fcw", (l * 3 + 2) * 44 + ft)
                        stt(acc[o + 1:o + 513], bank.full(), w1, acc[o + 1:o + 513], ALU.mult, ALU.add)
                        stt(acc[o:o + 512], bank.full(), w2, acc[o:o + 512], ALU.mult, ALU.add)
                    sg = SG[it % 4]
                    it += 1

                    def tail(f=f, cs=cs, o=o, sg=sg, a0=accs[0], a1=accs[1], last=(s == NSUB - 1)):
                        act(sg.full(), a1[o:o + 512], AF.Silu)
                        if last:
                            copy(POOL, FSTATE[l, f], a0[T:T + 2])
                            copy(POOL, FSTATE[l, f + FT], a1[T:T + 2])
                        tt(POOL, U[f, cs], a0[o:o + 512], sg.full(), ALU.mult)
                    tails.append(tail)
        for fn in tails:
            fn()
        for s in range(NSUB):
            cs = slice(s * NS, (s + 1) * NS)
            for dch in range(8):
                slot = get_chunk(("dn", l, dch))
                bank = next_bank()
                mm_group(bank.full(), [(wview(slot, f, 0, 128, 128), U[f, cs]) for f in range(FT)])
                stat_tick()
                resid_add(dch, s, bank)
                if s == 1 and dch == 1:
                    norm_emit(0, next_kind)
        norm_emit(1, next_kind)

    def w_out_proj(kind, l, RHS, next_kind):
        slots = [get_chunk((kind, l // 2, c)) for c in range(2)]
        for s in range(NSUB):
            cs = slice(s * NS, (s + 1) * NS)
            for c in range(2):
                for m in range(4):
                    bank = next_bank()
                    mm_group(bank.full(), [(wview(slots[c], k, m * 128, m * 128 + 128, 512), RHS[k, cs]) for k in range(KT)])
                    stat_tick()
                    resid_add(c * 4 + m, s, bank)
                    if s == 1 and c == 0 and m == 2:
                        norm_emit(0, next_kind)
        norm_emit(1, next_kind)

    def run_hook(hook):
        if hook[0] is not None:
            hook[0]()
            hook[0] = None

    def even_mixer(l, first_in_seq, next_kind, hook):
        i = l // 2
        ZA = big((4, PH + T), F32, 0)
        PT = [big((PH + T,), F32, 16640 + j * 4160) for j in range(2)]
        PA = big((4, T), BF16, 24960)
        GL = big((4, CH + T), BF16, 33152)
        MIX = H
        T1 = [tmp((NS,), F32, j * 2048) for j in range(2)]
        SIG = T1
        CSQT = tmp((4, T), BF16, 4096)
        CBT = big((4, T), F32, 0)
        CBBT = big((4, T), BF16, 16640)
        for g in range(4):
            if first_in_seq:
                memset(POOL, ZA[g, 0:PH], 0.0)
                memset(POOL, GL[g, 0:CH], 0.0)
            else:
                copy(POOL, ZA[g, 0:PH], ZSTATE[i, g])
                copy(POOL, GL[g, 0:CH], GSTATE[i, g])
        W = PH + T
        pool_ops = []

        def P(fn, *a):
            pool_ops.append(lambda: fn(*a))
        for g in range(4):
            for st in range(g + 1):
                sh = 1 << st
                dst = PT[st % 2]
                a = ZA[g, sh:W] if st == 0 else PT[(st - 1) % 2][sh:W]
                b = ZA[g, 0:W - sh] if st == 0 else PT[(st - 1) % 2][0:W - sh]
                P(tt, DVE, dst[sh:W], a, b, ALU.add)
            fin = PT[g % 2]
            w = 2 << g
            P(stt, PA[g], fin[PH:W], 1.0 / w, ZA[g, PH:W], ALU.mult, ALU.subtract)
            if first_in_seq:
                ne = w - 1
                P(tt, DVE, fin[PH:PH + ne], fin[PH:PH + ne], RCNT[0:ne], ALU.mult)
                P(tt, DVE, PA[g, 0:ne], fin[PH:PH + ne], ZA[g, PH:PH + ne], ALU.subtract)
            P(copy, POOL, ZSTATE[i, g], ZA[g, T:T + PH])
        for s in range(NSUB):
            cs = slice(s * NS, (s + 1) * NS)
            if s == 1:
                run_hook(hook)
            slot = get_chunk(("ev_in", i, 0))
            for g in range(4):
                bank = next_bank()
                mm_group(bank.full(), [(wview(slot, k, g * 128, g * 128 + 128, 512), H[k, cs]) for k in range(KT)])
                act(ZA[g, PH + s * NS:PH + (s + 1) * NS], bank.full(), AF.Copy)

            for c in (1, 2):
                slot = get_chunk(("ev_in", i, c))
                for j in range(2):
                    ct = (c - 1) * 2 + j
                    bv = next_bank()
                    mm_group(bv.full(), [(wview(slot, k, j * 128, j * 128 + 128, 512), H[k, cs]) for k in range(KT)])
                    bg = next_bank()
                    mm_group(bg.full(), [(wview(slot, k, 256 + j * 128, 256 + j * 128 + 128, 512), H[k, cs]) for k in range(KT)])
                    sg = SIG[j % 2]
                    act(sg.full(), bg.full(), AF.Sigmoid)
                    tt(DVE, GL[ct, CH + s * NS:CH + (s + 1) * NS], bv.full(), sg.full(), ALU.mult)
                    if s == NSUB - 1:
                        for _ in range(3):
                            if pool_ops:
                                pool_ops.pop(0)()
        while pool_ops:
            pool_ops.pop(0)()
        for g in range(4):
            copy(POOL, GSTATE[i, g], GL[g, T:T + CH])

        def ln_chain(s):
            cs = slice(s * NS, (s + 1) * NS)
            bm = next_bank()
            bq = next_bank()
            mm_group(bm.full(), [(ONESH.full(), CBBT[c, cs]) for c in range(4)])
            mm_group(bq.full(), [(ONESH.full(), CSQT[c, cs]) for c in range(4)])
            act(T1[0].full(), bm.full(), AF.Square)
            tt(DVE, T1[0].full(), bq.full(), T1[0].full(), ALU.subtract)
            act(bq.full(), T1[0].full(), AF.Ln, bias=EPSV.full())
            act(bq.full(), bq.full(), AF.Exp, scale=-0.5)
            for c in range(4):
                t1 = T1[c % 2]
                tt(DVE, t1.full(), CBT[c, cs], bm.full(), ALU.subtract)
                tt(DVE, t1.full(), t1.full(), bq.full(), ALU.mult)
                act(MIX[4 + c, cs], t1.full(), AF.Silu, scale=vcol("cn_g", i * 4 + c), bias=vcol("cn_b", i * 4 + c))

        for s in range(NSUB):
            cs = slice(s * NS, (s + 1) * NS)
            for c in range(4):
                slot = get_chunk(("diag", i, c))
                bb = vcol("conv_b", i * 4 + c)
                bank = next_bank()
                mm_group(bank.full(), [(slot[k * 128:(k + 1) * 128], GL[c, s * NS + k:s * NS + k + NS]) for k in range(CW)])
                act(CBT[c, cs], bank.full(), AF.Identity, bias=bb)
                act(CBBT[c, cs], bank.full(), AF.Identity, bias=bb)
                act(CSQT[c, cs], bank.full(), AF.Square, bias=bb)
                if s == 0 and c == 1:
                    pslot = get_chunk(("poolw", i))
                    for s2 in range(NSUB):
                        cs2 = slice(s2 * NS, (s2 + 1) * NS)
                        for g in range(4):
                            bk = next_bank()
                            mm_group(bk.full(), [(pslot[g * 128:(g + 1) * 128], PA[g, cs2])])
                            act(MIX[g, cs2], bk.full(), AF.Identity, scale=vcol("pool_scale", i * 4 + g))
                if s == 1 and c == 0:
                    ln_chain(0)
        ln_chain(1)
        w_out_proj("ev_out", l, MIX, next_kind)

    def odd_mixer(l, next_kind, hook):
        i = l // 2
        UU = big((KT, T), F32, 0)
        VN = big((T // 128, D), BF16, 32768)
        VT = [tmp((D,), F32, j * 4096) for j in range(3)]
        BNSs = [tmp((12,), F32, 12288 + j * 128) for j in range(2)]
        MVs = [tmp((2,), F32, 12288 + 64 + j * 128) for j in range(2)]
        RSs = [tmp((1,), F32, 12288 + 96 + j * 128) for j in range(2)]
        G = H
        dma(POOL, VNB[0], vn_d[i, 0].partition_broadcast(128), "vnb0")
        dma(POOL, VNB[1], vn_d[i, 1].partition_broadcast(128), "vnb1")
        for c in range(2):
            slot = get_chunk(("od_in", i, c))
            for s in range(NSUB):
                cs = slice(s * NS, (s + 1) * NS)
                if s == 1:
                    run_hook(hook)
                for m in range(4):
                    bank = next_bank()
                    mm_group(bank.full(), [(wview(slot, k, m * 128, m * 128 + 128, 512), H[k, cs]) for k in range(KT)])
                    act(UU[c * 4 + m, cs], bank.full(), AF.Gelu)
        sl = [get_chunk(("od_in", i, 2)), get_chunk(("od_in", i, 3))]
        pend_norm = []
        for tc in range(T // 128):
            ts_ = slice(tc * 128, (tc + 1) * 128)
            vt = VT[tc % 3]
            BNS, MV, RS = BNSs[tc % 2], MVs[tc % 2], RSs[tc % 2]
            for hh in range(2):
                bank = next_bank()
                mm_group(bank.full(), [(H[k, ts_], wview(sl[hh], k, 0, 512, 512)) for k in range(KT)])
                act(vt[hh * 512:(hh + 1) * 512], bank.full(), AF.Gelu)
                B.op(DVE, (lambda o, a: (lambda e: e.bn_stats(out=o.ap, in_=a.ap)))(BNS[hh * 6:hh * 6 + 6], vt[hh * 512:(hh + 1) * 512]),
                     reads=(vt[hh * 512:(hh + 1) * 512],), writes=(BNS[hh * 6:hh * 6 + 6],))
            B.op(DVE, (lambda MV, BNS: (lambda e: e.bn_aggr(out=MV.full().ap, in_=BNS.full().ap)))(MV, BNS),
                 reads=(BNS.full(),), writes=(MV.full(),))
            ts(DVE, RS.full(), MV[1:2], EPS, ALU.add)
            tt(POOL, RS.full(), RS.full(), MHALF[0:1], ALU.pow)
            for fn in pend_norm:
                fn()

            def norm(tc=tc, vt=vt, MV=MV, RS=RS):
                stt(vt.full(), vt.full(), MV[0:1], VNB[0], ALU.subtract, ALU.mult)
                stt(VN[tc], vt.full(), RS.full(), VNB[1], ALU.mult, ALU.add)
            pend_norm = [norm]
        for fn in pend_norm:
            fn()
        for s in range(NSUB):
            cs = slice(s * NS, (s + 1) * NS)
            for c in range(KT):
                h = c // 2
                bank = next_bank()
                ones_r = ONE1.rows(0, 1)[0:128]
                bias_r = BSROW.rows(0, 1)[(i * 4 + h) * 512:(i * 4 + h + 1) * 512]
                mains = [(VN[s * (NS // 128) + q, c * 128:(c + 1) * 128], WST[i, h], bank[q * 128:(q + 1) * 128])
                         for q in range(NS // 128)]

                def fn(e, ones_r=ones_r, bias_r=bias_r, mains=mains, bank=bank):
                    e.matmul(bank.full().ap, lhsT=ones_r.ap, rhs=bias_r.ap, start=True, stop=False)
                    ins = None
                    for qi, (l_, r_, o_) in enumerate(mains):
                        ins = e.matmul(o_.ap, lhsT=l_.ap, rhs=r_.ap, start=False, stop=(qi == len(mains) - 1))
                    return ins
                B.op(PE, fn, reads=[ones_r, bias_r] + [m[0] for m in mains] + [m[1] for m in mains], writes=(bank.full(),))
                tt(DVE, G[c, cs], bank.full(), UU[c, cs], ALU.mult)
        w_out_proj("od_out", l, G, next_kind)

    for ti in range(n_tiles):
        t0 = ti * T
        first = (ti % tiles_per_seq == 0)
        phases = []
        for l in cfg.layers:
            if has_mix:
                phases.append(("mix", l))
            if has_ffn:
                phases.append(("ffn", l))
        kinds = [("H", "mix_g" if p == "mix" else "ffn_g", l) for p, l in phases]
        kinds.append(("end", ti))
        if ti == 0:
            for s in range(NSUB):
                dma(POOL, X[:, s * NS:(s + 1) * NS], xT[:, :, t0 + s * NS:t0 + (s + 1) * NS], "xld%d" % s)
                for k in range(KT):
                    stat_update(k, s)
        first_kind[0] = kinds[0]
        if ti == 0:
            norm_emit(0, kinds[0])
        else:
            rest = [ent for ent in late_ops if ent[2] == 0]
            if rest:
                stat_flush()
            for ent in rest:
                ent[1]()
            late_ops[:] = [ent for ent in late_ops if ent[2] != 0]

        def pre_s1(k0=kinds[0]):
            late_flush()
            norm_emit(1, k0)
        for pi, (p, l) in enumerate(phases):
            nk = kinds[pi + 1]
            hook = [pre_s1 if pi == 0 else None]
            if p == "mix":
                if l % 2 == 0:
                    even_mixer(l, first, nk, hook)
                else:
                    odd_mixer(l, nk, hook)
            else:
                ffn_layer(l, first, nk, hook)
    B.wait_for(POOL, (X.full(), Y1.full()))

    print("arena used", cursor[0])
    B.emit(sems, dma_sems, block)
    es.close()
    print("instructions:", B.n_inst)
    return nc


def _kmajor(w, cols):
    return np.ascontiguousarray(w[:, cols].reshape(KT, 128, len(cols)).transpose(1, 0, 2))


def prep_weights(inp):
    f = np.float32
    vecs = np.zeros((128, NVEC), f)

    def put(name, arr):
        vecs[:, VOFF[name]:VOFF[name] + arr.shape[1]] = arr

    def pt(v):
        v = np.asarray(v, f)
        lead = v.shape[:-1]
        n = v.shape[-1] // 128
        return v.reshape(*lead, n, 128).reshape(-1, 128).T

    put("mix_g", pt(inp["mix_norm_g"]))
    put("ffn_g", pt(inp["ffn_norm_g"]))
    put("fin_g", pt(inp["final_norm_g"]))
    put("pool_scale", pt(inp["ev_pool_scale"]))
    put("conv_b", pt(inp["ev_conv_b"]))
    put("cn_g", pt(inp["ev_cn_g"]))
    put("cn_b", pt(inp["ev_cn_b"]))
    cw = np.asarray(inp["ev_conv_w"], f)
    cw = cw.reshape(2, CW, 4, 128).transpose(3, 0, 2, 1)
    put("conv_w", cw.reshape(128, -1))
    fw = np.asarray(inp["ffn_conv_w"], f)
    fw = fw.reshape(4, 3, 44, 128).transpose(3, 0, 1, 2)
    put("fcw", fw.reshape(128, -1))
    fb = np.asarray(inp["ffn_conv_b"], f).reshape(4, 44, 128).transpose(2, 0, 1)
    put("fcb", fb.reshape(128, -1))

    out = {"vecs": vecs}
    out["vn"] = np.ascontiguousarray(np.stack([inp["od_vn_g"], inp["od_vn_b"]], axis=1).astype(f))
    out["bs"] = np.ascontiguousarray(np.asarray(inp["od_b_s"], f).reshape(1, 1024))
    ws = np.asarray(inp["od_w_s"], f)
    out["wsT"] = np.ascontiguousarray(ws.transpose(0, 3, 1, 2).reshape(2, 128, 512))
    pw = np.asarray(inp["ev_pool_w"], f)
    out["poolw"] = np.ascontiguousarray(pw.transpose(0, 2, 1, 3).reshape(2, 128, 512))
    ar = np.arange
    ev_in = np.zeros((2, 3, 128, 4096), f)
    ev_out = np.zeros((2, 2, 128, 4096), f)
    od_in = np.zeros((2, 4, 128, 4096), f)
    od_out = np.zeros((2, 2, 128, 4096), f)
    for i in range(2):
        w = np.asarray(inp["ev_w_in"][i], f)
        ev_in[i, 0] = _kmajor(w, ar(0, 512)).reshape(128, -1)
        ev_in[i, 1] = _kmajor(w, np.concatenate([ar(512, 768), ar(1024, 1280)])).reshape(128, -1)
        ev_in[i, 2] = _kmajor(w, np.concatenate([ar(768, 1024), ar(1280, 1536)])).reshape(128, -1)
        w = np.asarray(inp["ev_w_out"][i], f)
        for c in range(2):
            ev_out[i, c] = _kmajor(w, ar(c * 512, c * 512 + 512)).reshape(128, -1)
        w = np.asarray(inp["od_w_in"][i], f)
        for c in range(4):
            od_in[i, c] = _kmajor(w, ar(c * 512, c * 512 + 512)).reshape(128, -1)
        w = np.asarray(inp["od_w_out"][i], f)
        for c in range(2):
            od_out[i, c] = _kmajor(w, ar(c * 512, c * 512 + 512)).reshape(128, -1)
    up = np.zeros((4, 11, 128, 4096), f)
    dn = np.zeros((4, 8, 128, 2816), f)
    for l in range(4):
        w = np.asarray(inp["ffn_w_up"][l], f)
        for c in range(11):
            cols = np.concatenate([ar(256 * c, 256 * c + 256), ar(DFF + 256 * c, DFF + 256 * c + 256)])
            up[l, c] = _kmajor(w, cols).reshape(128, -1)
        w = np.asarray(inp["ffn_w_down"][l], f)
        wd = w.reshape(FT, 128, D).transpose(1, 0, 2)
        for dch in range(8):
            dn[l, dch] = wd[:, :, dch * 128:(dch + 1) * 128].reshape(128, -1)
    out.update(ev_in=ev_in, ev_out=ev_out, od_in=od_in, od_out=od_out, ffn_up=up, ffn_dn=dn)
    return out


def shard_x(x, n_cores, n_seq):
    Bsz, S, _ = x.shape
    res = []
    for c in range(n_cores):
        xc = np.asarray(x[c * n_seq:(c + 1) * n_seq], np.float32).reshape(n_seq * S, KT, 128)
        res.append(np.ascontiguousarray(xc.transpose(2, 1, 0)))
    return res


def unshard_out(outs, n_seq, S):
    res = []
    for o in outs:
        res.append(o.transpose(2, 1, 0).reshape(n_seq, S, D))
    return np.ascontiguousarray(np.concatenate(res, axis=0)).astype(np.float32)


_NC_CACHE = {}


def run(inputs, cfg, n_cores, trace=False):
    key = (cfg.n_seq, cfg.seq, cfg.layers, cfg.final_norm, cfg.parts)
    if key not in _NC_CACHE:
        _NC_CACHE[key] = build_program(cfg)
    nc = _NC_CACHE[key]
    w = prep_weights(inputs)
    xs = shard_x(inputs["x"], n_cores, cfg.n_seq)
    in_maps = []
    for c in range(n_cores):
        m = dict(w)
        m["xT"] = xs[c]
        in_maps.append(m)
    res = run_bass_kernel_spmd(nc, in_maps, core_ids=list(range(n_cores)), trace=trace)
    outs = [r["outT"] for r in res.results]
    return unshard_out(outs, cfg.n_seq, cfg.seq), res


def kernel(**inputs):
    cfg = Cfg(n_seq=2, seq=4096)
    out, _ = run(inputs, cfg, 8)
    return out
```

```python
import bisect
import numpy as np
import concourse.bass as bass
import concourse.mybir as mybir
from concourse.bass_utils import run_bass_kernel_spmd

F32 = mybir.dt.float32
BF16 = mybir.dt.bfloat16
AF = mybir.ActivationFunctionType
ALU = mybir.AluOpType

PE, ACT, DVE, POOL, SP = "pe", "act", "dve", "pool", "sp"
ENGINES = (PE, ACT, DVE, POOL, SP)

D = 1024
KT = 8
DFF = 2816
FT = 22
EPS = 1e-6
T = 1024
NS = 512
NSUB = T // NS
CW = 31
PH = 15
CH = 30
RING_SLOTS = 4
SLOT_BYTES = 8192


class IMap:
    def __init__(self):
        self.st = []
        self.segs = []

    def _split(self, pos):
        i = bisect.bisect_right(self.st, pos) - 1
        if i >= 0 and self.st[i] < pos < self.segs[i][0]:
            end, w, r = self.segs[i]
            self.segs[i] = [pos, w, dict(r)]
            self.st.insert(i + 1, pos)
            self.segs.insert(i + 1, [end, w, dict(r)])

    def access(self, lo, hi, tok, write, deps):
        self._split(lo)
        self._split(hi)
        i = bisect.bisect_left(self.st, lo)
        j = i
        n = len(self.st)
        while j < n and self.st[j] < hi:
            seg = self.segs[j]
            if seg[1] is not None:
                deps.append(seg[1])
            if write:
                for k, v in seg[2].items():
                    deps.append((k, v))
            j += 1
        if write:
            del self.st[i:j]
            del self.segs[i:j]
            self.st.insert(i, lo)
            self.segs.insert(i, [hi, tok, {}])
        else:
            pos = lo
            k = i
            while k < j:
                s = self.st[k]
                if s > pos:
                    self.st.insert(k, pos)
                    self.segs.insert(k, [s, None, {tok[0]: tok[1]}])
                    k += 1
                    j += 1
                seg = self.segs[k]
                rd = seg[2]
                if rd.get(tok[0], -1) < tok[1]:
                    rd[tok[0]] = tok[1]
                pos = seg[0]
                k += 1
            if pos < hi:
                self.st.insert(k, pos)
                self.segs.insert(k, [hi, None, {tok[0]: tok[1]}])


    def collect(self, lo, hi, deps):
        i = bisect.bisect_right(self.st, lo) - 1
        if i < 0:
            i = 0
        n = len(self.st)
        while i < n and self.st[i] < hi:
            seg = self.segs[i]
            if seg[0] > lo:
                if seg[1] is not None:
                    deps.append(seg[1])
                for k, v in seg[2].items():
                    deps.append((k, v))
            i += 1


class V:
    __slots__ = ("ap", "space", "ivs")

    def __init__(self, ap, space, ivs):
        self.ap = ap
        self.space = space
        self.ivs = ivs


class Tl:
    def __init__(self, ap, space, off, shape, es):
        self.ap = ap
        self.space = space
        self.off = off
        self.shape = tuple(shape)
        self.es = es
        self.nbytes = int(np.prod(shape)) * es

    def __getitem__(self, idx):
        if not isinstance(idx, tuple):
            idx = (idx,)
        idx = list(idx) + [slice(None)] * (len(self.shape) - len(idx))
        rng = []
        for d, ix in zip(self.shape, idx):
            if isinstance(ix, slice):
                lo = 0 if ix.start is None else ix.start
                hi = d if ix.stop is None else ix.stop
                assert 0 <= lo < hi <= d, (lo, hi, d, self.shape)
                rng.append((lo, hi))
            else:
                assert 0 <= ix < d, (ix, d)
                rng.append((ix, ix + 1))
        ap = self.ap[(slice(None),) + tuple(idx)]
        strides = []
        s = self.es
        for d in reversed(self.shape):
            strides.append(s)
            s *= d
        strides = strides[::-1]
        nd = len(self.shape)
        k = nd
        while k > 0 and rng[k - 1] == (0, self.shape[k - 1]):
            k -= 1
        if k == 0:
            ivs = [(self.off, self.off + self.nbytes)]
        else:
            ivs = []
            inner = strides[k - 1]
            lo, hi = rng[k - 1]

            def rec(dim, base):
                if dim == k - 1:
                    ivs.append((base + lo * inner, base + hi * inner))
                    return
                for i in range(rng[dim][0], rng[dim][1]):
                    rec(dim + 1, base + i * strides[dim])
            rec(0, self.off)
        return V(ap, self.space, ivs)

    def full(self):
        return V(self.ap, self.space, [(self.off, self.off + self.nbytes)])

    def rows(self, lo, hi):
        t = Tl(self.ap[lo:hi], self.space, self.off, self.shape, self.es)
        return t


class Op:
    __slots__ = ("eng", "fn", "deps", "milestone", "dma_sem", "dma_n", "idx")


class Builder:
    def __init__(self, nc):
        self.nc = nc
        self.ops = {e: [] for e in ENGINES}
        self.maps = {}
        self.dma_counts = {}

    def _imap(self, space):
        m = self.maps.get(space)
        if m is None:
            m = self.maps[space] = IMap()
        return m

    def op(self, eng, fn, reads=(), writes=(), dma_sem=None, dma_n=0):
        o = Op()
        o.eng = eng
        o.fn = fn
        o.idx = len(self.ops[eng])
        o.milestone = False
        o.dma_sem = dma_sem
        o.dma_n = dma_n
        if dma_sem is not None:
            c = self.dma_counts.get(dma_sem, 0) + 16 * dma_n
            self.dma_counts[dma_sem] = c
            tok = (("dma", dma_sem), c)
        else:
            tok = (("eng", eng), o.idx)
        deps = []
        for v in reads:
            m = self._imap(v.space)
            for lo, hi in v.ivs:
                m.access(lo, hi, tok, False, deps)
        for v in writes:
            m = self._imap(v.space)
            for lo, hi in v.ivs:
                m.access(lo, hi, tok, True, deps)
        red = {}
        for k, val in deps:
            if k == tok[0] and val == tok[1]:
                continue
            if red.get(k, -1) < val:
                red[k] = val
        if dma_sem is None and eng == PE:
            red.pop(("eng", PE), None)
        o.deps = red
        self.ops[eng].append(o)
        return o

    def wait_for(self, eng, views):
        o = Op()
        o.eng = eng
        o.fn = None
        o.idx = len(self.ops[eng])
        o.milestone = False
        o.dma_sem = None
        o.dma_n = 0
        deps = []
        for v in views:
            m = self._imap(v.space)
            for lo, hi in v.ivs:
                m.collect(lo, hi, deps)
        red = {}
        for k, val in deps:
            if red.get(k, -1) < val:
                red[k] = val
        o.deps = red
        self.ops[eng].append(o)
        return o

    def emit(self, sems, dma_sems, block):
        for e in ENGINES:
            for o in self.ops[e]:
                for k, val in o.deps.items():
                    if k[0] == "eng":
                        self.ops[k[1]][val].milestone = True
        mcount = {}
        for e in ENGINES:
            c = 0
            arr = []
            for o in self.ops[e]:
                if o.milestone:
                    c += 1
                arr.append(c)
            mcount[e] = arr
        self.n_inst = {e: len(self.ops[e]) for e in ENGINES}

        def run(e, eng):
            waited = {}
            for o in self.ops[e]:
                for k, val in o.deps.items():
                    if k[0] == "eng":
                        sem = sems[k[1]]
                        target = mcount[k[1]][val]
                    else:
                        sem = dma_sems[k[1]]
                        target = val
                    if waited.get(k, 0) >= target:
                        continue
                    waited[k] = target
                    eng.wait_ge(sem, target)
                if o.fn is None:
                    continue
                ins = o.fn(eng)
                if o.dma_sem is None and o.milestone:
                    ins.then_inc(sems[e], 1)

        @block.tensor
        def _(eng):
            run(PE, eng)

        @block.scalar
        def _(eng):
            run(ACT, eng)

        @block.vector
        def _(eng):
            run(DVE, eng)

        @block.gpsimd
        def _(eng):
            run(POOL, eng)

        @block.sync
        def _(eng):
            run(SP, eng)


class Cfg:
    def __init__(self, n_seq=2, seq=4096, layers=(0, 1, 2, 3), final_norm=True, parts=("mix", "ffn")):
        self.n_seq = n_seq
        self.seq = seq
        self.ntok = n_seq * seq
        self.layers = tuple(layers)
        self.final_norm = final_norm
        self.parts = parts
        assert seq % T == 0


def _vec_layout():
    off = {}
    c = 0
    for name, n in (("mix_g", 32), ("ffn_g", 32), ("fin_g", 8), ("pool_scale", 8), ("conv_b", 8),
                    ("cn_g", 8), ("cn_b", 8), ("conv_w", 2 * 4 * CW), ("fcw", 4 * 3 * 44), ("fcb", 4 * 44)):
        off[name] = c
        c += n
    return off, c


VOFF, NVEC = _vec_layout()


def build_program(cfg):
    nc = bass.Bass("TRN2", target_bir_lowering=False)
    NTOK = cfg.ntok
    dt = nc.dram_tensor
    xT = dt("xT", [128, KT, NTOK], F32, kind="ExternalInput").ap()
    outT = dt("outT", [128, KT, NTOK], F32, kind="ExternalOutput").ap()
    vecs_d = dt("vecs", [128, NVEC], F32, kind="ExternalInput").ap()
    vn_d = dt("vn", [2, 2, D], F32, kind="ExternalInput").ap()
    bs_d = dt("bs", [1, 1024], F32, kind="ExternalInput").ap()
    wsT_d = dt("wsT", [2, 128, 512], F32, kind="ExternalInput").ap()
    poolw_d = dt("poolw", [2, 128, 512], F32, kind="ExternalInput").ap()
    ev_in_d = dt("ev_in", [2, 3, 128, 4096], F32, kind="ExternalInput").ap()
    ev_out_d = dt("ev_out", [2, 2, 128, 4096], F32, kind="ExternalInput").ap()
    od_in_d = dt("od_in", [2, 4, 128, 4096], F32, kind="ExternalInput").ap()
    od_out_d = dt("od_out", [2, 2, 128, 4096], F32, kind="ExternalInput").ap()
    up_d = dt("ffn_up", [4, 11, 128, 4096], F32, kind="ExternalInput").ap()
    dn_d = dt("ffn_dn", [4, 8, 128, 2816], F32, kind="ExternalInput").ap()
    ev_in_s = dt("ev_in_s", [2, 3, 128, 4096], BF16, kind="Internal").ap()
    ev_out_s = dt("ev_out_s", [2, 2, 128, 4096], BF16, kind="Internal").ap()
    od_in_s = dt("od_in_s", [2, 4, 128, 4096], BF16, kind="Internal").ap()
    od_out_s = dt("od_out_s", [2, 2, 128, 4096], BF16, kind="Internal").ap()
    up_s = dt("ffn_up_s", [4, 11, 128, 4096], BF16, kind="Internal").ap()
    dn_s = dt("ffn_dn_s", [4, 8, 128, 2816], BF16, kind="Internal").ap()
    diag_s = dt("diag_s", [2, 4, 128, CW * 128], BF16, kind="Internal").ap()
    poolw_s = dt("poolw_s", [2, 128, 512], BF16, kind="Internal").ap()

    ARENA_BYTES = 207 * 1024
    import contextlib
    es = contextlib.ExitStack()
    arena = es.enter_context(nc.sbuf_tensor("arena", [128, ARENA_BYTES // 4], F32))
    psum = es.enter_context(nc.psum_tensor("psum", [128, 8, 512], F32))
    sems = {e: es.enter_context(nc.semaphore("sem_" + e)) for e in (PE, ACT, DVE, POOL)}
    DMA_SEM_NAMES = ["ring%d" % i for i in range(RING_SLOTS)] + ["xld0", "xld1", "yst0", "yst1", "vec", "wst0", "wst1", "bsr", "vnb0", "vnb1", "sto0", "sto1", "sto2", "sto3"]
    dma_sems = {n: es.enter_context(nc.semaphore("dsem_" + n)) for n in DMA_SEM_NAMES}
    block = es.enter_context(nc.Block())

    B = Builder(nc)
    cursor = [0]

    def alloc(shape, dtype, at=None):
        esz = 4 if dtype == F32 else 2
        nb = int(np.prod(shape)) * esz
        if at is None:
            off = (cursor[0] + 31) // 32 * 32
            cursor[0] = off + nb
            assert cursor[0] <= ARENA_BYTES, ("arena overflow", cursor[0])
        else:
            off = at
        assert off % 4 == 0 and nb % 4 == 0
        ap = arena[:, off // 4:(off + nb) // 4]
        if dtype == BF16:
            ap = ap.bitcast(BF16)
        if len(shape) == 2:
            ap = ap.rearrange("p (a b) -> p a b", a=shape[0])
        elif len(shape) == 3:
            ap = ap.rearrange("p (a b c) -> p a b c", a=shape[0], b=shape[1])
        return Tl(ap, "sb", off, shape, esz)

    banks = [Tl(psum[:, b, :], "ps", b * 2048, (512,), 4) for b in range(8)]
    bank_rr = [0]

    def next_bank():
        b = banks[bank_rr[0] % 6]
        bank_rr[0] += 1
        return b
    SB0, SB1 = banks[6], banks[7]

    X = alloc((KT, T), F32)
    H = alloc((KT, T), BF16)
    BIG_OFF = (cursor[0] + 31) // 32 * 32
    BIG_BYTES = 48 * 1024
    cursor[0] = BIG_OFF + BIG_BYTES
    RING = [alloc((SLOT_BYTES // 2,), BF16) for _ in range(RING_SLOTS)]
    TMP_OFF = (cursor[0] + 31) // 32 * 32
    TMP_BYTES = 25 * 1024
    cursor[0] = TMP_OFF + TMP_BYTES
    SQ = alloc((KT, NS), BF16)
    VEC = alloc((NVEC,), F32)
    ONES = alloc((128,), BF16)
    ONESH = alloc((128,), BF16)
    ONE1 = alloc((128,), BF16)
    MHALF = alloc((8,), F32)
    RCNT = alloc((16,), F32)
    ZSTATE = alloc((2, 4, PH), F32)
    GSTATE = alloc((2, 4, CH), BF16)
    FSTATE = alloc((4, 44, 2), F32)
    WST = alloc((2, 4, 128), BF16)
    BSROW = alloc((4096,), BF16)
    VNB = alloc((2, D), F32)
    STAT = alloc((16,), F32)


    def big(shape, dtype, boff):
        return alloc(shape, dtype, at=BIG_OFF + boff)

    def tmp(shape, dtype, toff):
        return alloc(shape, dtype, at=TMP_OFF + toff)

    def vcol(name, idx):
        c = VOFF[name] + idx
        return VEC[c:c + 1]

    def dma(eng, out_v, src_ap, sem):
        B.op(eng, lambda e: e.dma_start(out=out_v.ap, in_=src_ap).then_inc(dma_sems[sem], 16),
             reads=(), writes=(out_v,), dma_sem=sem, dma_n=1)

    def dma_out(eng, dst_ap, in_v, sem):
        B.op(eng, lambda e: e.dma_start(out=dst_ap, in_=in_v.ap).then_inc(dma_sems[sem], 16),
             reads=(in_v,), writes=(), dma_sem=sem, dma_n=1)

    def act(out_v, in_v, func, scale=None, bias=None, extra_reads=()):
        kw = {}
        rd = [in_v] + list(extra_reads)
        if scale is not None:
            if isinstance(scale, V):
                kw["scale"] = scale.ap
                rd.append(scale)
            else:
                kw["scale"] = scale
        if bias is not None:
            if isinstance(bias, V):
                kw["bias"] = bias.ap
                rd.append(bias)
            else:
                kw["bias"] = bias
        B.op(ACT, lambda e: e.activation(out=out_v.ap, in_=in_v.ap, func=func, **kw), reads=rd, writes=(out_v,))

    def tt(eng, out_v, a_v, b_v, op):
        B.op(eng, lambda e: e.tensor_tensor(out=out_v.ap, in0=a_v.ap, in1=b_v.ap, op=op),
             reads=(a_v, b_v), writes=(out_v,))

    def ts(eng, out_v, a_v, s1, op0, s2=None, op1=None):
        rd = [a_v]
        a1 = s1
        if isinstance(s1, V):
            a1 = s1.ap
            rd.append(s1)
        a2 = s2
        if isinstance(s2, V):
            a2 = s2.ap
            rd.append(s2)
        if op1 is None:
            B.op(eng, lambda e: e.tensor_scalar(out=out_v.ap, in0=a_v.ap, scalar1=a1, scalar2=None, op0=op0),
                 reads=rd, writes=(out_v,))
        else:
            B.op(eng, lambda e: e.tensor_scalar(out=out_v.ap, in0=a_v.ap, scalar1=a1, scalar2=a2, op0=op0, op1=op1),
                 reads=rd, writes=(out_v,))

    def stt(out_v, a_v, s, b_v, op0, op1):
        rd = [a_v, b_v]
        a1 = s
        if isinstance(s, V):
            a1 = s.ap
            rd.append(s)
        B.op(DVE, lambda e: e.scalar_tensor_tensor(out=out_v.ap, in0=a_v.ap, scalar=a1, in1=b_v.ap, op0=op0, op1=op1),
             reads=rd, writes=(out_v,))

    def copy(eng, out_v, in_v):
        B.op(eng, lambda e: e.tensor_copy(out=out_v.ap, in_=in_v.ap), reads=(in_v,), writes=(out_v,))

    def memset(eng, out_v, val):
        B.op(eng, lambda e: e.memset(out_v.ap, val), reads=(), writes=(out_v,))

    def mm_group(out_v, pairs, extra_reads=(), fine=False):
        n = len(pairs)
        if fine:
            for i, (l, r) in enumerate(pairs):
                B.op(PE, (lambda l, r, i: (lambda e: e.matmul(out_v.ap, lhsT=l.ap, rhs=r.ap, start=(i == 0), stop=(i == n - 1))))(l, r, i),
                     reads=(l, r), writes=(out_v,))
            return
        rd = []
        for l, r in pairs:
            rd.append(l)
            rd.append(r)
        rd += list(extra_reads)

        def fn(e):
            ins = None
            for i, (l, r) in enumerate(pairs):
                ins = e.matmul(out_v.ap, lhsT=l.ap, rhs=r.ap, start=(i == 0), stop=(i == n - 1))
            return ins
        B.op(PE, fn, reads=rd, writes=(out_v,))

    n_tiles = NTOK // T
    tiles_per_seq = cfg.seq // T
    has_mix = "mix" in cfg.parts
    has_ffn = "ffn" in cfg.parts
    chunk_ids = {}

    def dram_v(key):
        cid = chunk_ids.setdefault(key, len(chunk_ids))
        return V(None, "dr", [(cid, cid + 1)])

    def wview(slot, k, m0, m1, ncols):
        a = k * ncols + m0
        return slot[a:a + (m1 - m0)]

    def build_diag(i, c):
        def build(slot):
            for k in range(CW):
                w = vcol("conv_w", (i * 4 + c) * CW + k)
                if k % 2 == 0:
                    ts(DVE, slot[k * 128:(k + 1) * 128], IDENT.full(), w, ALU.mult)
                else:
                    act(slot[k * 128:(k + 1) * 128], IDENT.full(), AF.Identity, scale=w)
        return build

    def layer_chunks(l):
        i = l // 2
        res = []
        if has_mix:
            if l % 2 == 0:
                for rep in range(NSUB):
                    for c in range(3):
                        res.append((("ev_in", i, c), ev_in_d[i, c], ev_in_s[i, c], 4096, None, rep == 0))
                for rep in range(NSUB):
                    for c in range(4):
                        res.append((("diag", i, c), None, diag_s[i, c], CW * 128, build_diag(i, c), rep == 0))
                        if rep == 0 and c == 1:
                            res.append((("poolw", i), poolw_d[i], poolw_s[i], 512, None, True))
                for c in range(2):
                    res.append((("ev_out", i, c), ev_out_d[i, c], ev_out_s[i, c], 4096, None, True))
            else:
                for c in range(4):
                    res.append((("od_in", i, c), od_in_d[i, c], od_in_s[i, c], 4096, None, True))
                for c in range(2):
                    res.append((("od_out", i, c), od_out_d[i, c], od_out_s[i, c], 4096, None, True))
        if has_ffn:
            for c in range(11):
                res.append((("up", l, c), up_d[l, c], up_s[l, c], 4096, None, True))
            for rep in range(NSUB):
                for c in range(8):
                    res.append((("dn", l, c), dn_d[l, c], dn_s[l, c], 2816, None, rep == 0))
        return res

    tile_chunks = []
    for l in cfg.layers:
        tile_chunks += layer_chunks(l)
    NCH = len(tile_chunks)
    issued = [0]
    consumed = [0]
    sto_i = [0]
    LOOKAHEAD = 2

    def issue_chunk(n):
        ti, ci = divmod(n, NCH)
        if ti >= n_tiles:
            return
        key, src_ap, scr_ap, nelem, build, do_store = tile_chunks[ci]
        s = n % RING_SLOTS
        slot = RING[s]
        v = slot[0:nelem]
        rs = "ring%d" % s
        half = nelem // 2 if nelem > 2048 else nelem
        nostore = not do_store
        if ti == 0:
            if build is None:
                o_ap = v.ap.rearrange("p (a b) -> p a b", b=half)
                i_ap = src_ap.rearrange("p (a b) -> p a b", b=half)
                B.op(POOL, lambda e: e.dma_start(out=o_ap, in_=i_ap).then_inc(dma_sems[rs], 16),
                     reads=(), writes=(v,), dma_sem=rs, dma_n=1)
            else:
                build(slot)
            if n_tiles > 1 and not nostore:
                j = sto_i[0] % 4
                sto_i[0] += 1
                name = "sto%d" % j
                B.op(SP, lambda e: e.dma_start(out=scr_ap, in_=v.ap).then_inc(dma_sems[name], 16),
                     reads=(v,), writes=(dram_v(key), dram_v(("sem", name))), dma_sem=name, dma_n=1)
        else:
            B.op(SP, lambda e: e.dma_start(out=v.ap, in_=scr_ap).then_inc(dma_sems[rs], 16),
                 reads=(dram_v(key),), writes=(v,), dma_sem=rs, dma_n=1)

    def get_chunk(key):
        n = consumed[0]
        consumed[0] += 1
        assert tile_chunks[n % NCH][0] == key, (tile_chunks[n % NCH][0], key)
        while issued[0] <= n + LOOKAHEAD:
            issue_chunk(issued[0])
            issued[0] += 1
        return RING[n % RING_SLOTS]

    dma(POOL, VEC.full(), vecs_d, "vec")
    memset(DVE, ONES.full(), 1.0 / 1024.0)
    memset(DVE, ONESH.full(), 1.0 / 512.0)
    memset(DVE, ONE1.full(), 1.0)
    memset(DVE, MHALF.full(), -0.5)
    for i in range(16):
        memset(DVE, RCNT[i:i + 1], 1.0 / (i + 1))
    EPSV = alloc((1,), F32)
    memset(DVE, EPSV.full(), EPS)
    IDENT = alloc((128,), F32)
    Y1 = alloc((KT, NS), F32)
    if has_mix:
        STGA = alloc((1024,), F32, at=BIG_OFF)
        ONESF = alloc((128,), F32, at=BIG_OFF + 4096)
        memset(POOL, ONESF.full(), 1.0)
        B.op(POOL, lambda e: e.affine_select(out=IDENT.full().ap, in_=ONESF.full().ap, pattern=[[1, 128]],
                                             compare_op=ALU.is_equal, fill=0.0, base=0, channel_multiplier=-1),
             reads=(ONESF.full(),), writes=(IDENT.full(),))
        for i in sorted(set(l // 2 for l in cfg.layers if l % 2 == 1)):
            dma(SP, STGA[0:512], wsT_d[i], "wst%d" % i)
            for h in range(4):
                src = STGA[h * 128:(h + 1) * 128]
                dstv = WST[i, h]
                B.op(POOL, (lambda src, dstv: (lambda e: e.affine_select(
                    out=dstv.ap, in_=src.ap, pattern=[[1, 128]], compare_op=ALU.is_ge, fill=0.0, base=0,
                    channel_multiplier=-1)))(src, dstv), reads=(src,), writes=(dstv,))
        r0 = STGA.rows(0, 1)
        dma(SP, r0[0:1024], bs_d, "bsr")
        for ih in range(8):
            for q in range(4):
                copy(DVE, BSROW.rows(0, 1)[ih * 512 + q * 128:ih * 512 + (q + 1) * 128], r0[ih * 128:(ih + 1) * 128])

    stat_cnt = [0, 0]
    SBS = [SB0, SB1]

    stat_pend = []

    def stat_update(k, s, defer=False):
        cs = slice(s * NS, (s + 1) * NS)
        sq = SQ[(k + 4 * s) % KT]
        act(sq, X[k, cs], AF.Square)
        c = stat_cnt[s]
        bank = SBS[s]

        def pe_part():
            B.op(PE, lambda e: e.matmul(bank.full().ap, lhsT=ONES.full().ap, rhs=sq.ap, start=(c == 0), stop=(c == KT - 1)),
                 reads=(ONES.full(), sq), writes=(bank.full(),))
        stat_cnt[s] = (c + 1) % KT
        if defer:
            stat_pend.append([int(defer), pe_part, s])
        else:
            pe_part()

    late_ops = []
    first_kind = [None]

    def late_flush():
        if late_ops:
            stat_flush()
        for ent in late_ops:
            ent[1]()
        late_ops[:] = []

    def stat_tick():
        keepl = []
        for ent in late_ops:
            ent[0] -= 1
            if ent[0] <= 0:
                stat_flush()
                ent[1]()
            else:
                keepl.append(ent)
        late_ops[:] = keepl
        keep = []
        for ent in stat_pend:
            ent[0] -= 1
            if ent[0] <= 0:
                ent[1]()
            else:
                keep.append(ent)
        stat_pend[:] = keep

    def stat_flush(s=None):
        keep = []
        for ent in stat_pend:
            if s is None or ent[2] == s:
                ent[1]()
            else:
                keep.append(ent)
        stat_pend[:] = keep

    def rstd_bank(s):
        stat_flush(s)
        assert stat_cnt[s] == 0
        bank = SBS[s]
        act(bank.full(), bank.full(), AF.Ln, bias=EPSV.full())
        act(bank.full(), bank.full(), AF.Exp, scale=-0.5)
        return bank

    def norm_emit(s, kind):
        cs = slice(s * NS, (s + 1) * NS)
        bank = rstd_bank(s)
        if kind[0] == "H":
            _, gname, l = kind
            for k in range(KT):
                stt(H[k, cs], X[k, cs], vcol(gname, l * 8 + k), bank.full(), ALU.mult, ALU.mult)
        else:
            ti = kind[1]
            t0 = ti * T
            if cfg.final_norm:
                for k in range(KT):
                    stt(Y1[k], X[k, cs], vcol("fin_g", k), bank.full(), ALU.mult, ALU.mult)
                src = Y1.full()
            else:
                src = X[:, cs]
            if ti + 1 < n_tiles:
                t1 = (ti + 1) * T
                dma(POOL, X[:, cs], xT[:, :, t1 + s * NS:t1 + (s + 1) * NS], "xld%d" % s)

                def upd(s=s):
                    for k in range(KT):
                        stat_update(k, s)
                late_ops.append([4, upd, s])
                if s == 0:
                    late_ops.append([6, lambda: norm_emit(0, first_kind[0]), 0])
            dma_out(POOL, outT[:, :, t0 + s * NS:t0 + (s + 1) * NS], src, "yst%d" % s)

    def resid_add(dch, s, bank):
        cs = slice(s * NS, (s + 1) * NS)
        tt(DVE, X[dch, cs], bank.full(), X[dch, cs], ALU.add)
        stat_update(dch, s, defer=2)

    def ffn_layer(l, first_in_seq, next_kind, hook):
        AW = T + 2
        ACCL = [tmp((AW,), F32, j * 4112) for j in range(4)]
        SG = [tmp((512,), F32, 4 * 4112 + j * 2048) for j in range(4)]
        U = big((FT, T), BF16, 0)
        tails = []
        it = 0
        for c in range(11):
            slot = get_chunk(("up", l, c))
            accs_f = {}
            for j in range(2):
                f = 2 * c + j
                accs = {}
                for part in (0, 1):
                    ft = f + part * FT
                    acc = ACCL[2 * j + part]
                    accs[part] = acc
                    bb = vcol("fcb", l * 44 + ft)
                    if first_in_seq:
                        copy(POOL, acc[0:1], bb)
                        copy(POOL, acc[1:2], bb)
                    else:
                        copy(POOL, acc[0:2], FSTATE[l, ft])
                accs_f[j] = accs
            for s in range(NSUB):
                if s == 1:
                    run_hook(hook)
                for j in range(2):
                    f = 2 * c + j
                    accs = accs_f[j]
                    cs = slice(s * NS, (s + 1) * NS)
                    o = s * NS
                    bks = {}
                    for part in (0, 1):
                        ft = f + part * FT
                        acc = accs[part]
                        bank = next_bank()
                        bks[part] = bank
                        m0 = part * 256 + j * 128
                        mm_group(bank.full(), [(wview(slot, k, m0, m0 + 128, 512), H[k, cs]) for k in range(KT)], fine=True)
                        w0 = vcol("fcw", (l * 3 + 0) * 44 + ft)
                        bb = vcol("fcb", l * 44 + ft)
                        act(acc[o + 2:o + 514], bank.full(), AF.Identity, scale=w0, bias=bb)
                        if part == 0:
                            for fn in tails:
                                fn()
                            tails = []
                    for part in (0, 1):
                        ft = f + part * FT
                        acc = accs[part]
                        bank = bks[part]
                        w1 = vcol("fcw", (l * 3 + 1) * 44 + ft)
                        w2 = vcol("fcw", (l * 3 + 2) * 44 + ft)
                        stt(acc[o + 1:o + 513], bank.full(), w1, acc[o + 1:o + 513], ALU.mult, ALU.add)
                        stt(acc[o:o + 512], bank.full(), w2, acc[o:o + 512], ALU.mult, ALU.add)
                    sg = SG[it % 4]
                    it += 1

                    def tail(f=f, cs=cs, o=o, sg=sg, a0=accs[0], a1=accs[1], last=(s == NSUB - 1)):
                        act(sg.full(), a1[o:o + 512], AF.Silu)
                        if last:
                            copy(POOL, FSTATE[l, f], a0[T:T + 2])
                            copy(POOL, FSTATE[l, f + FT], a1[T:T + 2])
                        tt(POOL, U[f, cs], a0[o:o + 512], sg.full(), ALU.mult)
                    tails.append(tail)
        for fn in tails:
            fn()
        for s in range(NSUB):
            cs = slice(s * NS, (s + 1) * NS)
            for dch in range(8):
                slot = get_chunk(("dn", l, dch))
                bank = next_bank()
                mm_group(bank.full(), [(wview(slot, f, 0, 128, 128), U[f, cs]) for f in range(FT)])
                stat_tick()
                resid_add(dch, s, bank)
                if s == 1 and dch == 1:
                    norm_emit(0, next_kind)
        norm_emit(1, next_kind)

    def w_out_proj(kind, l, RHS, next_kind):
        slots = [get_chunk((kind, l // 2, c)) for c in range(2)]
        for s in range(NSUB):
            cs = slice(s * NS, (s + 1) * NS)
            for c in range(2):
                for m in range(4):
                    bank = next_bank()
                    mm_group(bank.full(), [(wview(slots[c], k, m * 128, m * 128 + 128, 512), RHS[k, cs]) for k in range(KT)])
                    stat_tick()
                    resid_add(c * 4 + m, s, bank)
                    if s == 1 and c == 0 and m == 2:
                        norm_emit(0, next_kind)
        norm_emit(1, next_kind)

    def run_hook(hook):
        if hook[0] is not None:
            hook[0]()
            hook[0] = None

    def even_mixer(l, first_in_seq, next_kind, hook):
        i = l // 2
        ZA = big((4, PH + T), F32, 0)
        PT = [big((PH + T,), F32, 16640 + j * 4160) for j in range(2)]
        PA = big((4, T), BF16, 24960)
        GL = big((4, CH + T), BF16, 33152)
        MIX = H
        T1 = [tmp((NS,), F32, j * 2048) for j in range(2)]
        SIG = T1
        CSQT = tmp((4, T), BF16, 4096)
        CBT = big((4, T), F32, 0)
        CBBT = big((4, T), BF16, 16640)
        for g in range(4):
            if first_in_seq:
                memset(POOL, ZA[g, 0:PH], 0.0)
                memset(POOL, GL[g, 0:CH], 0.0)
            else:
                copy(POOL, ZA[g, 0:PH], ZSTATE[i, g])
                copy(POOL, GL[g, 0:CH], GSTATE[i, g])
        W = PH + T
        pool_ops = []

        def P(fn, *a):
            pool_ops.append(lambda: fn(*a))
        for g in range(4):
            for st in range(g + 1):
                sh = 1 << st
                dst = PT[st % 2]
                a = ZA[g, sh:W] if st == 0 else PT[(st - 1) % 2][sh:W]
                b = ZA[g, 0:W - sh] if st == 0 else PT[(st - 1) % 2][0:W - sh]
                P(tt, DVE, dst[sh:W], a, b, ALU.add)
            fin = PT[g % 2]
            w = 2 << g
            P(stt, PA[g], fin[PH:W], 1.0 / w, ZA[g, PH:W], ALU.mult, ALU.subtract)
            if first_in_seq:
                ne = w - 1
                P(tt, DVE, fin[PH:PH + ne], fin[PH:PH + ne], RCNT[0:ne], ALU.mult)
                P(tt, DVE, PA[g, 0:ne], fin[PH:PH + ne], ZA[g, PH:PH + ne], ALU.subtract)
            P(copy, POOL, ZSTATE[i, g], ZA[g, T:T + PH])
        for s in range(NSUB):
            cs = slice(s * NS, (s + 1) * NS)
            if s == 1:
                run_hook(hook)
            slot = get_chunk(("ev_in", i, 0))
            for g in range(4):
                bank = next_bank()
                mm_group(bank.full(), [(wview(slot, k, g * 128, g * 128 + 128, 512), H[k, cs]) for k in range(KT)], fine=True)
                act(ZA[g, PH + s * NS:PH + (s + 1) * NS], bank.full(), AF.Copy)

            for c in (1, 2):
                slot = get_chunk(("ev_in", i, c))
                for j in range(2):
                    ct = (c - 1) * 2 + j
                    bv = next_bank()
                    mm_group(bv.full(), [(wview(slot, k, j * 128, j * 128 + 128, 512), H[k, cs]) for k in range(KT)], fine=True)
                    bg = next_bank()
                    mm_group(bg.full(), [(wview(slot, k, 256 + j * 128, 256 + j * 128 + 128, 512), H[k, cs]) for k in range(KT)], fine=True)
                    sg = SIG[j % 2]
                    act(sg.full(), bg.full(), AF.Sigmoid)
                    tt(DVE, GL[ct, CH + s * NS:CH + (s + 1) * NS], bv.full(), sg.full(), ALU.mult)
                    if s == NSUB - 1:
                        for _ in range(3):
                            if pool_ops:
                                pool_ops.pop(0)()
        while pool_ops:
            pool_ops.pop(0)()
        for g in range(4):
            copy(POOL, GSTATE[i, g], GL[g, T:T + CH])

        def ln_chain(s):
            cs = slice(s * NS, (s + 1) * NS)
            bm = next_bank()
            bq = next_bank()
            mm_group(bm.full(), [(ONESH.full(), CBBT[c, cs]) for c in range(4)])
            mm_group(bq.full(), [(ONESH.full(), CSQT[c, cs]) for c in range(4)])
            act(T1[0].full(), bm.full(), AF.Square)
            tt(DVE, T1[0].full(), bq.full(), T1[0].full(), ALU.subtract)
            act(bq.full(), T1[0].full(), AF.Ln, bias=EPSV.full())
            act(bq.full(), bq.full(), AF.Exp, scale=-0.5)
            for c in range(4):
                t1 = T1[c % 2]
                tt(DVE, t1.full(), CBT[c, cs], bm.full(), ALU.subtract)
                tt(DVE, t1.full(), t1.full(), bq.full(), ALU.mult)
                act(MIX[4 + c, cs], t1.full(), AF.Silu, scale=vcol("cn_g", i * 4 + c), bias=vcol("cn_b", i * 4 + c))

        for s in range(NSUB):
            cs = slice(s * NS, (s + 1) * NS)
            for c in range(4):
                slot = get_chunk(("diag", i, c))
                bb = vcol("conv_b", i * 4 + c)
                bank = next_bank()
                mm_group(bank.full(), [(slot[k * 128:(k + 1) * 128], GL[c, s * NS + k:s * NS + k + NS]) for k in range(CW)])
                act(CBT[c, cs], bank.full(), AF.Identity, bias=bb)
                act(CBBT[c, cs], bank.full(), AF.Identity, bias=bb)
                act(CSQT[c, cs], bank.full(), AF.Square, bias=bb)
                if s == 0 and c == 1:
                    pslot = get_chunk(("poolw", i))
                    for s2 in range(NSUB):
                        cs2 = slice(s2 * NS, (s2 + 1) * NS)
                        for g in range(4):
                            bk = next_bank()
                            mm_group(bk.full(), [(pslot[g * 128:(g + 1) * 128], PA[g, cs2])])
                            act(MIX[g, cs2], bk.full(), AF.Identity, scale=vcol("pool_scale", i * 4 + g))
                if s == 1 and c == 0:
                    ln_chain(0)
        ln_chain(1)
        w_out_proj("ev_out", l, MIX, next_kind)

    def odd_mixer(l, next_kind, hook):
        i = l // 2
        UU = big((KT, T), F32, 0)
        VN = big((T // 128, D), BF16, 32768)
        VT = [tmp((D,), F32, j * 4096) for j in range(3)]
        BNSs = [tmp((12,), F32, 12288 + j * 128) for j in range(2)]
        MVs = [tmp((2,), F32, 12288 + 64 + j * 128) for j in range(2)]
        RSs = [tmp((1,), F32, 12288 + 96 + j * 128) for j in range(2)]
        G = H
        dma(POOL, VNB[0], vn_d[i, 0].partition_broadcast(128), "vnb0")
        dma(POOL, VNB[1], vn_d[i, 1].partition_broadcast(128), "vnb1")
        for c in range(2):
            slot = get_chunk(("od_in", i, c))
            for s in range(NSUB):
                cs = slice(s * NS, (s + 1) * NS)
                if s == 1:
                    run_hook(hook)
                for m in range(4):
                    bank = next_bank()
                    mm_group(bank.full(), [(wview(slot, k, m * 128, m * 128 + 128, 512), H[k, cs]) for k in range(KT)], fine=True)
                    act(UU[c * 4 + m, cs], bank.full(), AF.Gelu)
        sl = [get_chunk(("od_in", i, 2)), get_chunk(("od_in", i, 3))]
        pend_norm = []
        for tc in range(T // 128):
            ts_ = slice(tc * 128, (tc + 1) * 128)
            vt = VT[tc % 3]
            BNS, MV, RS = BNSs[tc % 2], MVs[tc % 2], RSs[tc % 2]
            for hh in range(2):
                bank = next_bank()
                mm_group(bank.full(), [(H[k, ts_], wview(sl[hh], k, 0, 512, 512)) for k in range(KT)], fine=True)
                act(vt[hh * 512:(hh + 1) * 512], bank.full(), AF.Gelu)
                B.op(DVE, (lambda o, a: (lambda e: e.bn_stats(out=o.ap, in_=a.ap)))(BNS[hh * 6:hh * 6 + 6], vt[hh * 512:(hh + 1) * 512]),
                     reads=(vt[hh * 512:(hh + 1) * 512],), writes=(BNS[hh * 6:hh * 6 + 6],))
            B.op(DVE, (lambda MV, BNS: (lambda e: e.bn_aggr(out=MV.full().ap, in_=BNS.full().ap)))(MV, BNS),
                 reads=(BNS.full(),), writes=(MV.full(),))
            ts(DVE, RS.full(), MV[1:2], EPS, ALU.add)
            tt(POOL, RS.full(), RS.full(), MHALF[0:1], ALU.pow)
            for fn in pend_norm:
                fn()

            def norm(tc=tc, vt=vt, MV=MV, RS=RS):
                stt(vt.full(), vt.full(), MV[0:1], VNB[0], ALU.subtract, ALU.mult)
                stt(VN[tc], vt.full(), RS.full(), VNB[1], ALU.mult, ALU.add)
            pend_norm = [norm]
        for fn in pend_norm:
            fn()
        for s in range(NSUB):
            cs = slice(s * NS, (s + 1) * NS)
            for c in range(KT):
                h = c // 2
                bank = next_bank()
                ones_r = ONE1.rows(0, 1)[0:128]
                bias_r = BSROW.rows(0, 1)[(i * 4 + h) * 512:(i * 4 + h + 1) * 512]
                mains = [(VN[s * (NS // 128) + q, c * 128:(c + 1) * 128], WST[i, h], bank[q * 128:(q + 1) * 128])
                         for q in range(NS // 128)]

                def fn(e, ones_r=ones_r, bias_r=bias_r, mains=mains, bank=bank):
                    e.matmul(bank.full().ap, lhsT=ones_r.ap, rhs=bias_r.ap, start=True, stop=False)
                    ins = None
                    for qi, (l_, r_, o_) in enumerate(mains):
                        ins = e.matmul(o_.ap, lhsT=l_.ap, rhs=r_.ap, start=False, stop=(qi == len(mains) - 1))
                    return ins
                B.op(PE, fn, reads=[ones_r, bias_r] + [m[0] for m in mains] + [m[1] for m in mains], writes=(bank.full(),))
                tt(DVE, G[c, cs], bank.full(), UU[c, cs], ALU.mult)
        w_out_proj("od_out", l, G, next_kind)

    for ti in range(n_tiles):
        t0 = ti * T
        first = (ti % tiles_per_seq == 0)
        phases = []
        for l in cfg.layers:
            if has_mix:
                phases.append(("mix", l))
            if has_ffn:
                phases.append(("ffn", l))
        kinds = [("H", "mix_g" if p == "mix" else "ffn_g", l) for p, l in phases]
        kinds.append(("end", ti))
        if ti == 0:
            for s in range(NSUB):
                dma(POOL, X[:, s * NS:(s + 1) * NS], xT[:, :, t0 + s * NS:t0 + (s + 1) * NS], "xld%d" % s)
                for k in range(KT):
                    stat_update(k, s)
        first_kind[0] = kinds[0]
        if ti == 0:
            norm_emit(0, kinds[0])
        else:
            rest = [ent for ent in late_ops if ent[2] == 0]
            if rest:
                stat_flush()
            for ent in rest:
                ent[1]()
            late_ops[:] = [ent for ent in late_ops if ent[2] != 0]

        def pre_s1(k0=kinds[0]):
            late_flush()
            norm_emit(1, k0)
        for pi, (p, l) in enumerate(phases):
            nk = kinds[pi + 1]
            hook = [pre_s1 if pi == 0 else None]
            if p == "mix":
                if l % 2 == 0:
                    even_mixer(l, first, nk, hook)
                else:
                    odd_mixer(l, nk, hook)
            else:
                ffn_layer(l, first, nk, hook)
    B.wait_for(POOL, (X.full(), Y1.full()))

    print("arena used", cursor[0])
    B.emit(sems, dma_sems, block)
    es.close()
    print("instructions:", B.n_inst)
    return nc


def _kmajor(w, cols):
    return np.ascontiguousarray(w[:, cols].reshape(KT, 128, len(cols)).transpose(1, 0, 2))


def prep_weights(inp):
    f = np.float32
    vecs = np.zeros((128, NVEC), f)

    def put(name, arr):
        vecs[:, VOFF[name]:VOFF[name] + arr.shape[1]] = arr

    def pt(v):
        v = np.asarray(v, f)
        lead = v.shape[:-1]
        n = v.shape[-1] // 128
        return v.reshape(*lead, n, 128).reshape(-1, 128).T

    put("mix_g", pt(inp["mix_norm_g"]))
    put("ffn_g", pt(inp["ffn_norm_g"]))
    put("fin_g", pt(inp["final_norm_g"]))
    put("pool_scale", pt(inp["ev_pool_scale"]))
    put("conv_b", pt(inp["ev_conv_b"]))
    put("cn_g", pt(inp["ev_cn_g"]))
    put("cn_b", pt(inp["ev_cn_b"]))
    cw = np.asarray(inp["ev_conv_w"], f)
    cw = cw.reshape(2, CW, 4, 128).transpose(3, 0, 2, 1)
    put("conv_w", cw.reshape(128, -1))
    fw = np.asarray(inp["ffn_conv_w"], f)
    fw = fw.reshape(4, 3, 44, 128).transpose(3, 0, 1, 2)
    put("fcw", fw.reshape(128, -1))
    fb = np.asarray(inp["ffn_conv_b"], f).reshape(4, 44, 128).transpose(2, 0, 1)
    put("fcb", fb.reshape(128, -1))

    out = {"vecs": vecs}
    out["vn"] = np.ascontiguousarray(np.stack([inp["od_vn_g"], inp["od_vn_b"]], axis=1).astype(f))
    out["bs"] = np.ascontiguousarray(np.asarray(inp["od_b_s"], f).reshape(1, 1024))
    ws = np.asarray(inp["od_w_s"], f)
    out["wsT"] = np.ascontiguousarray(ws.transpose(0, 3, 1, 2).reshape(2, 128, 512))
    pw = np.asarray(inp["ev_pool_w"], f)
    out["poolw"] = np.ascontiguousarray(pw.transpose(0, 2, 1, 3).reshape(2, 128, 512))
    ar = np.arange
    ev_in = np.zeros((2, 3, 128, 4096), f)
    ev_out = np.zeros((2, 2, 128, 4096), f)
    od_in = np.zeros((2, 4, 128, 4096), f)
    od_out = np.zeros((2, 2, 128, 4096), f)
    for i in range(2):
        w = np.asarray(inp["ev_w_in"][i], f)
        ev_in[i, 0] = _kmajor(w, ar(0, 512)).reshape(128, -1)
        ev_in[i, 1] = _kmajor(w, np.concatenate([ar(512, 768), ar(1024, 1280)])).reshape(128, -1)
        ev_in[i, 2] = _kmajor(w, np.concatenate([ar(768, 1024), ar(1280, 1536)])).reshape(128, -1)
        w = np.asarray(inp["ev_w_out"][i], f)
        for c in range(2):
            ev_out[i, c] = _kmajor(w, ar(c * 512, c * 512 + 512)).reshape(128, -1)
        w = np.asarray(inp["od_w_in"][i], f)
        for c in range(4):
            od_in[i, c] = _kmajor(w, ar(c * 512, c * 512 + 512)).reshape(128, -1)
        w = np.asarray(inp["od_w_out"][i], f)
        for c in range(2):
            od_out[i, c] = _kmajor(w, ar(c * 512, c * 512 + 512)).reshape(128, -1)
    up = np.zeros((4, 11, 128, 4096), f)
    dn = np.zeros((4, 8, 128, 2816), f)
    for l in range(4):
        w = np.asarray(inp["ffn_w_up"][l], f)
        for c in range(11):
            cols = np.concatenate([ar(256 * c, 256 * c + 256), ar(DFF + 256 * c, DFF + 256 * c + 256)])
            up[l, c] = _kmajor(w, cols).reshape(128, -1)
        w = np.asarray(inp["ffn_w_down"][l], f)
        wd = w.reshape(FT, 128, D).transpose(1, 0, 2)
        for dch in range(8):
            dn[l, dch] = wd[:, :, dch * 128:(dch + 1) * 128].reshape(128, -1)
    out.update(ev_in=ev_in, ev_out=ev_out, od_in=od_in, od_out=od_out, ffn_up=up, ffn_dn=dn)
    return out


def shard_x(x, n_cores, n_seq):
    Bsz, S, _ = x.shape
    res = []
    for c in range(n_cores):
        xc = np.asarray(x[c * n_seq:(c + 1) * n_seq], np.float32).reshape(n_seq * S, KT, 128)
        res.append(np.ascontiguousarray(xc.transpose(2, 1, 0)))
    return res


def unshard_out(outs, n_seq, S):
    res = []
    for o in outs:
        res.append(o.transpose(2, 1, 0).reshape(n_seq, S, D))
    return np.ascontiguousarray(np.concatenate(res, axis=0)).astype(np.float32)


_NC_CACHE = {}


def run(inputs, cfg, n_cores, trace=False):
    key = (cfg.n_seq, cfg.seq, cfg.layers, cfg.final_norm, cfg.parts)
    if key not in _NC_CACHE:
        _NC_CACHE[key] = build_program(cfg)
    nc = _NC_CACHE[key]
    w = prep_weights(inputs)
    xs = shard_x(inputs["x"], n_cores, cfg.n_seq)
    in_maps = []
    for c in range(n_cores):
        m = dict(w)
        m["xT"] = xs[c]
        in_maps.append(m)
    res = run_bass_kernel_spmd(nc, in_maps, core_ids=list(range(n_cores)), trace=trace)
    outs = [r["outT"] for r in res.results]
    return unshard_out(outs, cfg.n_seq, cfg.seq), res


def kernel(**inputs):
    cfg = Cfg(n_seq=2, seq=4096)
    out, _ = run(inputs, cfg, 8)
    return out
```
